# Optimizing a Trainium2 kernel written in Bass

```python
import jax, jax.numpy as jnp
from jax import lax
import numpy as np

D_MODEL = 1024
BATCH = 8
SEQ = 2048
DEPTH = 4

GRID_W = 64
CTX_LEN = 256
D_MIX = 1024
FOUR_GROUPS = 4
FOUR_DIM = 64
D_FOUR = FOUR_GROUPS * FOUR_DIM
NA_HEADS = 6
NA_HEAD_DIM = 64
D_NA = NA_HEADS * NA_HEAD_DIM
NA_KH = 8
NA_KW = 16
GLA_HEADS = 4
GLA_DK = 48
GLA_DV = 96
D_GLA_K = GLA_HEADS * GLA_DK
D_GLA_V = GLA_HEADS * GLA_DV
GLA_RANK = 16
GLA_TAU = 16.0
GLA_CHUNK = 64
ROPE_BASE = 10000.0
EPS = 1e-6

IN_SIZES = (D_FOUR, D_FOUR, D_NA, D_NA, D_NA, D_NA, D_GLA_K, D_GLA_K, D_GLA_V, D_GLA_V, GLA_RANK, GLA_RANK)
D_IN = sum(IN_SIZES)
IN_SPLITS = tuple(int(s) for s in np.cumsum(IN_SIZES)[:-1])

kernel_name = "hybrid_fourier_natten_gla_dit"


def rmsnorm(x, w):
    xf = x.astype(jnp.float32)
    y = xf * lax.rsqrt(jnp.mean(xf * xf, axis=-1, keepdims=True) + EPS)
    return (y * w.astype(jnp.float32)).astype(x.dtype)


def split_heads(t, h):
    return t.reshape(t.shape[:-1] + (h, t.shape[-1] // h))


def rope_2d(t):
    L = t.shape[1]
    pos = jnp.arange(L)
    row = (pos // GRID_W).astype(jnp.float32)
    col = (pos % GRID_W).astype(jnp.float32)
    half = t.shape[-1] // 2
    n_freq = half // 2
    inv = ROPE_BASE ** (-jnp.arange(n_freq, dtype=jnp.float32) / n_freq)

    def rot(u, p):
        ang = p[:, None] * inv[None, :]
        cos = jnp.cos(ang)[None, :, None, :]
        sin = jnp.sin(ang)[None, :, None, :]
        u1, u2 = u[..., :n_freq], u[..., n_freq:]
        return jnp.concatenate([u1 * cos - u2 * sin, u1 * sin + u2 * cos], axis=-1)

    tf = t.astype(jnp.float32)
    out = jnp.concatenate([rot(tf[..., :half], row), rot(tf[..., half:], col)], axis=-1)
    return out.astype(t.dtype)


def fourier_mix(u, w_four):
    B, L, _ = u.shape
    ug = u.astype(jnp.float32).reshape(B, L, FOUR_GROUPS, FOUR_DIM)
    f = jnp.fft.fft2(ug, axes=(1, 3), norm="ortho").real
    return f.reshape(B, L, D_FOUR).astype(u.dtype) @ w_four


def neighbourhood_attention(q, k, v, kc, vc, rpb):
    B, L, H, d = q.shape
    rows = L // GRID_W
    kh = min(NA_KH, rows)
    r = np.arange(rows)
    row_start = np.clip(r - kh // 2, 0, rows - kh)
    row_idx = row_start[:, None] + np.arange(kh)[None, :]
    cq = np.arange(GRID_W)
    col_start = np.clip(cq - NA_KW // 2, 0, GRID_W - NA_KW)
    col_mask = (cq[None, :] >= col_start[:, None]) & (cq[None, :] < col_start[:, None] + NA_KW)
    row_off = row_idx - r[:, None] + NA_KH - 1
    col_off = np.clip(cq[None, :] - cq[:, None] + NA_KW - 1, 0, 2 * NA_KW - 2)
    bias = rpb[:, row_off[:, None, :, None], col_off[None, :, None, :]].astype(jnp.float32)
    bias = jnp.where(col_mask[None, None, :, None, :], bias, -jnp.inf)

    scale = d ** -0.5
    qg = q.reshape(B, rows, GRID_W, H, d) * scale
    kg = k.reshape(B, rows, GRID_W, H, d)[:, row_idx]
    vg = v.reshape(B, rows, GRID_W, H, d)[:, row_idx]
    s_loc = jnp.einsum('brqhd,brikhd->bhrqik', qg, kg).astype(jnp.float32) + bias[None]
    s_loc = s_loc.reshape(B, H, rows, GRID_W, kh * GRID_W)
    s_ctx = jnp.einsum('brqhd,bchd->bhrqc', qg, kc).astype(jnp.float32)
    p = jax.nn.softmax(jnp.concatenate([s_loc, s_ctx], axis=-1), axis=-1)
    p_loc = p[..., :kh * GRID_W].reshape(B, H, rows, GRID_W, kh, GRID_W).astype(v.dtype)
    p_ctx = p[..., kh * GRID_W:].astype(v.dtype)
    o = jnp.einsum('bhrqik,brikhd->brqhd', p_loc, vg) + jnp.einsum('bhrqc,bchd->brqhd', p_ctx, vc)
    return o.reshape(B, L, H * d)


def context_attention(qc, kc, vc):
    B, Lc, H, d = qc.shape
    s = jnp.einsum('bqhd,bkhd->bhqk', qc, kc).astype(jnp.float32) * (d ** -0.5)
    p = jax.nn.softmax(s, axis=-1).astype(vc.dtype)
    return jnp.einsum('bhqk,bkhd->bqhd', p, vc).reshape(B, Lc, H * d)


def gla_scan(q, k, v, log_a, s0):
    B, L, H, dk = q.shape
    dv = v.shape[-1]
    n = L // GLA_CHUNK

    def chunks(t):
        return t.astype(jnp.float32).reshape(B, n, GLA_CHUNK, H, t.shape[-1]).transpose(1, 0, 3, 2, 4)

    tri = jnp.tril(jnp.ones((GLA_CHUNK, GLA_CHUNK), dtype=bool))

    def step(state, inp):
        qi, ki, vi, gi = inp
        b = jnp.cumsum(gi, axis=2)
        b_end = b[:, :, -1:, :]
        q_dec = qi * jnp.exp(b)
        k_dec = ki * jnp.exp(-b)
        a = jnp.where(tri, jnp.einsum('bhtk,bhsk->bhts', q_dec, k_dec), 0.0)
        o = jnp.einsum('bhts,bhsv->bhtv', a, vi) + jnp.einsum('bhtk,bhkv->bhtv', q_dec, state)
        k_to_end = ki * jnp.exp(b_end - b)
        state = jnp.exp(b_end[:, :, 0, :])[..., None] * state + jnp.einsum('bhsk,bhsv->bhkv', k_to_end, vi)
        return state, o

    state, o = lax.scan(step, s0.astype(jnp.float32), (chunks(q), chunks(k), chunks(v), chunks(log_a)))
    o = o.transpose(1, 0, 3, 2, 4).reshape(B, L, H, dv)
    return o, state


def gla_log_decay(z, w_a, b_a):
    g = (z @ w_a + b_a).astype(jnp.float32)
    return split_heads(jax.nn.log_sigmoid(g) / GLA_TAU, GLA_HEADS)


def gla_out_norm(o, w, dtype):
    o = o * lax.rsqrt(jnp.mean(o * o, axis=-1, keepdims=True) + EPS) * w.astype(jnp.float32)
    return o.reshape(o.shape[0], o.shape[1], D_GLA_V).astype(dtype)


def setup_inputs(seed: int = 0) -> dict:
    key = jax.random.key(seed)
    ks = jax.random.split(key, 20)
    f32 = jnp.float32
    nrm = lambda k, shape, s: jax.random.normal(k, shape, f32) * s
    return {
        "x": nrm(ks[0], (BATCH, SEQ, D_MODEL), 1.0),
        "c": nrm(ks[1], (BATCH, D_MODEL), 1.0),
        "ctx": nrm(ks[2], (BATCH, CTX_LEN, D_MODEL), 1.0),
        "c_ctx": nrm(ks[3], (D_MODEL,), 1.0),
        "w_ada": nrm(ks[4], (DEPTH, D_MODEL, 3 * D_MODEL), 0.5 * D_MODEL ** -0.5),
        "b_ada": nrm(ks[5], (DEPTH, 3 * D_MODEL), 0.02),
        "norm_w": 1.0 + nrm(ks[6], (DEPTH, D_MODEL), 0.02),
        "w_in": nrm(ks[7], (DEPTH, D_MODEL, D_IN), D_MODEL ** -0.5),
        "w_four": nrm(ks[8], (DEPTH, D_FOUR, D_FOUR), D_FOUR ** -0.5),
        "rpb": nrm(ks[9], (DEPTH, NA_HEADS, 2 * NA_KH - 1, 2 * NA_KW - 1), 0.1),
        "w_alpha_fwd": nrm(ks[10], (DEPTH, GLA_RANK, D_GLA_K), GLA_RANK ** -0.5),
        "b_alpha_fwd": nrm(ks[11], (DEPTH, D_GLA_K), 0.1),
        "w_alpha_bwd": nrm(ks[12], (DEPTH, GLA_RANK, D_GLA_K), GLA_RANK ** -0.5),
        "b_alpha_bwd": nrm(ks[13], (DEPTH, D_GLA_K), 0.1),
        "gla_norm_w": 1.0 + nrm(ks[14], (DEPTH, GLA_DV), 0.02),
        "w_out": nrm(ks[15], (DEPTH, D_MIX, D_MODEL), D_MIX ** -0.5),
        "norm_f": 1.0 + nrm(ks[16], (D_MODEL,), 0.02),
    }


def reference(x, c, ctx, c_ctx, w_ada, b_ada, norm_w, w_in, w_four, rpb,
              w_alpha_fwd, b_alpha_fwd, w_alpha_bwd, b_alpha_bwd, gla_norm_w, w_out, norm_f):
    silu_c = jax.nn.silu(c)
    silu_cc = jax.nn.silu(c_ctx)
    for l in range(DEPTH):
        last = l == DEPTH - 1
        shift_x, scale_x, gate_x = jnp.split(silu_c @ w_ada[l] + b_ada[l], 3, axis=-1)
        shift_c, scale_c, gate_c = jnp.split(silu_cc @ w_ada[l] + b_ada[l], 3, axis=-1)
        hx = rmsnorm(x, norm_w[l]) * (1.0 + scale_x[:, None, :]) + shift_x[:, None, :]
        hc = rmsnorm(ctx, norm_w[l]) * (1.0 + scale_c) + shift_c

        (fx, fgx, nqx, nkx, nvx, ngx, gqx, gkx, gvx, ggx, zfx, zbx) = jnp.split(hx @ w_in[l], IN_SPLITS, axis=-1)
        (fc, fgc, nqc, nkc, nvc, ngc, gqc, gkc, gvc, ggc, zfc, zbc) = jnp.split(hc @ w_in[l], IN_SPLITS, axis=-1)

        nkc_h, nvc_h = split_heads(nkc, NA_HEADS), split_heads(nvc, NA_HEADS)
        na_x = neighbourhood_attention(split_heads(nqx, NA_HEADS), split_heads(nkx, NA_HEADS),
                                       split_heads(nvx, NA_HEADS), nkc_h, nvc_h, rpb[l])

        qx_g = rope_2d(split_heads(gqx, GLA_HEADS)) * (GLA_DK ** -0.5)
        kx_g = rope_2d(split_heads(gkx, GLA_HEADS))
        vx_g = split_heads(gvx, GLA_HEADS)
        qc_g = split_heads(gqc, GLA_HEADS) * (GLA_DK ** -0.5)
        kc_g = split_heads(gkc, GLA_HEADS)
        vc_g = split_heads(gvc, GLA_HEADS)
        la_xf = gla_log_decay(zfx, w_alpha_fwd[l], b_alpha_fwd[l])
        la_cf = gla_log_decay(zfc, w_alpha_fwd[l], b_alpha_fwd[l])
        la_xb = gla_log_decay(zbx, w_alpha_bwd[l], b_alpha_bwd[l])
        la_cb = gla_log_decay(zbc, w_alpha_bwd[l], b_alpha_bwd[l])
        s0 = jnp.zeros((x.shape[0], GLA_HEADS, GLA_DK, GLA_DV), jnp.float32)
        flip = lambda t: jnp.flip(t, axis=1)
        oc_f, sc_f = gla_scan(qc_g, kc_g, vc_g, la_cf, s0)
        ox_f, _ = gla_scan(qx_g, kx_g, vx_g, la_xf, sc_f)
        oc_b, sc_b = gla_scan(flip(qc_g), flip(kc_g), flip(vc_g), flip(la_cb), s0)
        ox_b, _ = gla_scan(flip(qx_g), flip(kx_g), flip(vx_g), flip(la_xb), sc_b)
        gla_x = gla_out_norm(ox_f + flip(ox_b), gla_norm_w[l], x.dtype)

        four_x = fourier_mix(fx, w_four[l])

        mix_x = jnp.concatenate([four_x * jax.nn.silu(fgx), na_x * jax.nn.silu(ngx),
                                 gla_x * jax.nn.silu(ggx)], axis=-1)
        x_new = x + gate_x[:, None, :] * (mix_x @ w_out[l])

        if not last:
            na_c = context_attention(split_heads(nqc, NA_HEADS), nkc_h, nvc_h)
            gla_c = gla_out_norm(oc_f + flip(oc_b), gla_norm_w[l], ctx.dtype)
            four_c = fourier_mix(fc, w_four[l])
            mix_c = jnp.concatenate([four_c * jax.nn.silu(fgc), na_c * jax.nn.silu(ngc),
                                     gla_c * jax.nn.silu(ggc)], axis=-1)
            ctx = ctx + gate_c * (mix_c @ w_out[l])
        x = x_new
    return rmsnorm(x, norm_f)
```

```python
import contextlib
import numpy as np
import ml_dtypes
import concourse.bass as bass
import concourse.mybir as mybir
from concourse.bass_utils import run_bass_kernel_spmd

F32 = mybir.dt.float32
BF16 = mybir.dt.bfloat16
AF = mybir.ActivationFunctionType
ALU = mybir.AluOpType
NPBF = ml_dtypes.bfloat16

D = 1024
L = 2048
LC = 256
NTOK = L + LC
NT = NTOK // 128
DIN = 3232
EPS = 1e-6
PAD_BIAS = -100.0
TBS = [(0, 512, 0), (512, 512, 0), (1024, 512, 0), (1536, 512, 0), (2048, 256, 1)]

ENGS = ("sp", "pool", "act", "dve", "pe")


class Buf:
    __slots__ = ("name", "writer", "readers")

    def __init__(self, name=""):
        self.name = name
        self.writer = None
        self.readers = []


class Op:
    __slots__ = ("eng", "fn", "deps", "signal", "sem", "val", "is_dma", "waits", "bar", "phase")

    def __init__(self, eng, fn, is_dma=False, bar=True):
        self.eng = eng
        self.fn = fn
        self.deps = []
        self.signal = False
        self.sem = None
        self.val = 0
        self.is_dma = is_dma
        self.waits = []
        self.bar = bar


class Prog:
    def __init__(self):
        self.ops = []
        self.last = {}
        self.pend = {e: [] for e in ENGS}
        self.bar_dmas = []
        self.phase = ''

    def op(self, eng, fn, reads=(), writes=(), is_dma=False, bar=True):
        o = Op(eng, fn, is_dma, bar)
        o.phase = self.phase
        deps = []
        for b in reads:
            if b.writer is not None:
                deps.append(b.writer)
        for b in writes:
            if b.writer is not None:
                deps.append(b.writer)
            deps.extend(b.readers)
        if self.pend[eng]:
            deps.extend(self.pend[eng])
            self.pend[eng] = []
        seen = set()
        for d in deps:
            if d is o or id(d) in seen:
                continue
            seen.add(id(d))
            if d.eng == "pe" and eng == "pe" and not d.is_dma and not is_dma:
                continue
            o.deps.append(d)
            d.signal = True
        for b in reads:
            if not is_dma:
                b.readers = [r for r in b.readers if r.is_dma or r.eng != eng]
            b.readers.append(o)
        for b in writes:
            b.writer = o
            b.readers = []
        self.ops.append(o)
        if is_dma:
            if bar:
                self.bar_dmas.append(o)
        else:
            self.last[eng] = o
        return o

    def dma(self, q, out, in_, reads=(), writes=(), bar=True):
        return self.op(q, lambda e: e.dma_start(out=out, in_=in_), reads, writes, is_dma=True, bar=bar)

    def barrier(self):
        deps = list(self.last.values()) + self.bar_dmas
        self.bar_dmas = []
        for e in ENGS:
            self.pend[e] = self.pend[e] + deps

    def emit(self, nc, n_hw_sems=16, n_sw_sems=32):
        ops = self.ops
        with contextlib.ExitStack() as es:
            esem = {e: es.enter_context(nc.semaphore("s_" + e)) for e in ENGS}
            hsems = [es.enter_context(nc.semaphore("h%d" % i)) for i in range(n_hw_sems)]
            ssems = [es.enter_context(nc.semaphore("w%d" % i)) for i in range(n_sw_sems)]
            block = es.enter_context(nc.Block())
            pos = {id(o): i for i, o in enumerate(ops)}
            cons = {}
            for o in ops:
                for d in o.deps:
                    if d.is_dma and d.eng == "pool":
                        cons.setdefault(id(d), []).append(o)
            hcount = [0] * n_hw_sems
            hlast = [None] * n_hw_sems
            slast = [None] * n_sw_sems
            sgen = [0] * n_sw_sems
            hi = si = 0
            clear_before = {}
            gen_of = {}
            all_dma = []
            for o in ops:
                if not o.is_dma:
                    continue
                o.signal = True
                all_dma.append(o)
                if o.eng == "pool":
                    k = si % n_sw_sems
                    si += 1
                    prev = slast[k]
                    if prev is not None and all(d is not prev for d in o.deps):
                        o.deps.append(prev)
                    sgen[k] += 16
                    o.sem = ssems[k]
                    o.val = sgen[k]
                    slast[k] = o
                else:
                    k = hi % n_hw_sems
                    hi += 1
                    if hlast[k] is not None and all(d is not hlast[k] for d in o.deps):
                        o.deps.append(hlast[k])
                    hcount[k] += 16
                    o.sem = hsems[k]
                    o.val = hcount[k]
                    hlast[k] = o
            cnt = {e: 0 for e in ENGS}
            for o in ops:
                if o.signal and not o.is_dma:
                    cnt[o.eng] += 1
                    o.sem = esem[o.eng]
                    o.val = cnt[o.eng]
            seen = {e: {} for e in ENGS}
            for o in ops:
                need = {}
                for d in o.deps:
                    key = id(d.sem)
                    if key not in need or need[key][1] < d.val:
                        need[key] = (d.sem, d.val)
                for key, (sem, val) in need.items():
                    if seen[o.eng].get(key, 0) >= val:
                        continue
                    seen[o.eng][key] = val
                    o.waits.append((sem, val))
            tail = {}
            for o in all_dma:
                tail[id(o.sem)] = (o.sem, o.val)

            def run(engname, eng):
                for o in ops:
                    if o.eng != engname:
                        continue
                    for sem, val in o.waits:
                        eng.wait_ge(sem, val)
                    if id(o) in clear_before:
                        eng.sem_clear(clear_before[id(o)])
                    ins = o.fn(eng)
                    if o.signal:
                        ins.then_inc(o.sem, 16 if o.is_dma else 1)
                if engname == "sp":
                    for sem, val in tail.values():
                        eng.wait_ge(sem, val)

            @block.sync
            def _(e):
                run("sp", e)

            @block.gpsimd
            def _(e):
                run("pool", e)

            @block.scalar
            def _(e):
                run("act", e)

            @block.vector
            def _(e):
                run("dve", e)

            @block.tensor
            def _(e):
                run("pe", e)
        return {e: sum(1 for o in ops if o.eng == e) for e in ENGS}, dict(cnt)


def _na_tiles():
    return ([(5, 5 + dk) for dk in (-2, -1, 0, 1, 2)] + [(0, k) for k in range(4)] + [(1, k) for k in range(4)]
            + [(14, k) for k in (12, 13, 14, 15)] + [(15, k) for k in (12, 13, 14, 15)])


def _na_keytiles(j):
    if j == 0:
        return [0, 1, 2, 3], 5
    if j == 1:
        return [0, 1, 2, 3], 9
    if j == 14:
        return [12, 13, 14, 15], 13
    if j == 15:
        return [12, 13, 14, 15], 17
    return [j - 2, j - 1, j, j + 1, j + 2], 0


def _na_bias_index():
    rows, kh = 32, 8
    rs = np.clip(np.arange(rows) - 4, 0, rows - kh)
    cs = np.clip(np.arange(64) - 8, 0, 48)
    tiles = _na_tiles()
    pad = 15 * 31
    idx = np.full((len(tiles), 128, 128), pad, np.int64)
    kc = np.arange(64)[:, None]
    qc = np.arange(64)[None, :]
    co = np.clip(kc - qc + 15, 0, 30)
    valid = (kc >= cs[qc]) & (kc < cs[qc] + 16)
    for t, (j, kt) in enumerate(tiles):
        for a in range(2):
            kr = 2 * kt + a
            for c in range(2):
                r = 2 * j + c
                if not (rs[r] <= kr < rs[r] + kh):
                    continue
                ro = kr - r + 7
                idx[t, a * 64:(a + 1) * 64, c * 64:(c + 1) * 64] = np.where(valid, ro * 31 + co, pad)
    return idx


def _win_pieces():
    ps = [((256, 256),), ((0, 256),), ((1664, 384),)]
    ps += [((512 + 128 * i, 128), (896 + 128 * i, 128), (1280 + 128 * i, 128)) for i in range(3)]
    ps += [((2816, 384),), ((2048, 384),), ((2432, 384),), ((3200, 32),)]
    lay, off = {}, 0
    for p in ps:
        n = sum(k for _, k in p)
        lay[p] = (off, n)
        off += 8 * n
    assert off == 8 * DIN
    return ps, lay


WOUT_BASE = {0: 0, 256: 2048, 640: 5120}

_CONST_CACHE = {}


def _consts():
    if _CONST_CACHE:
        return _CONST_CACHE
    C = _CONST_CACHE
    C["ident"] = np.eye(128, dtype=np.float32).astype(NPBF)
    k = np.arange(64)
    ang = 2 * np.pi * np.outer(k, k) / 64.0
    c64 = np.cos(ang) / 8.0
    s64 = -np.sin(ang) / 8.0
    blk = np.zeros((128, 256), np.float64)
    for g in range(2):
        blk[g * 64:(g + 1) * 64, g * 64:(g + 1) * 64] = c64
        blk[g * 64:(g + 1) * 64, 128 + g * 64:128 + (g + 1) * 64] = s64
    C["dft64"] = blk.astype(np.float32).astype(NPBF)

    def dftn(n):
        t = np.arange(n, dtype=np.int64)
        m = np.outer(t, t) % n
        a = 2 * np.pi * m.astype(np.float64) / n
        return np.stack([np.cos(a), np.sin(a)]).astype(np.float32) / np.float32(np.sqrt(n))
    C["dftL"] = dftn(L).astype(NPBF)
    dc = dftn(LC).astype(NPBF)
    C["dftC"] = np.ascontiguousarray(dc.reshape(2, 2, 128, 256).transpose(2, 1, 0, 3))
    pos = np.arange(L)
    row = (pos // 64).astype(np.float32)
    col = (pos % 64).astype(np.float32)
    inv = (10000.0 ** (-np.arange(12, dtype=np.float32) / 12)).astype(np.float32)
    ar = row[:, None] * inv[None, :]
    ac = col[:, None] * inv[None, :]
    cos48 = np.concatenate([np.cos(ar), np.cos(ar), np.cos(ac), np.cos(ac)], -1)
    sin48 = np.concatenate([-np.sin(ar), np.sin(ar), -np.sin(ac), np.sin(ac)], -1)
    C["ropeC"] = np.ascontiguousarray(cos48.reshape(16, 128, 48).transpose(1, 0, 2)).astype(np.float32)
    C["ropeS"] = np.ascontiguousarray(sin48.reshape(16, 128, 48).transpose(1, 0, 2)).astype(np.float32)
    s = np.arange(128)[:, None]
    t = np.arange(128)[None, :]
    tri = np.stack([(s <= t), (s >= t)]).astype(np.float32)
    C["tri"] = np.ascontiguousarray(tri.transpose(1, 0, 2))
    m4 = np.repeat(tri[:, :, None, :], 4, axis=2)
    C["mask4"] = np.ascontiguousarray(m4.transpose(1, 0, 2, 3)).astype(NPBF)
    C["naidx"] = _na_bias_index()
    return C


def build(n_layers=4, taps=(), skip=()):
    nc = bass.Bass("TRN2", target_bir_lowering=False)

    def din(name, shape, dt=F32):
        return nc.dram_tensor(name, list(shape), dt, kind="ExternalInput").ap()

    xT_d = din("xT", [D, NTOK])
    cvec_d = din("cvec", [128, 16])
    w_ada_d = din("w_ada_r", [4, 128, 8 * 3 * D])
    b_ada_d = din("b_ada", [4, 128, 24])
    norm_w_d = din("norm_w", [4, 128, 8])
    w_in_d = din("w_in_r", [4, 128, 8 * DIN])
    WIN_LAY = _win_pieces()[1]
    w_four_d = din("w_four", [4, 256, 256])
    biasT_d = din("biasT", [4, 6, 128, 21, 128])
    walpha_d = din("walpha", [4, 33, 384])
    gnw_d = din("gnw", [4, 96, 1])
    w_out_d = din("w_out_r", [4, 128, 9216])
    normf_d = din("normf", [128, 8])
    ident_d = din("ident", [128, 128], BF16)
    dft64_d = din("dft64", [128, 256], BF16)
    dftL_d = din("dftL", [2, L, L], BF16)
    dftC_d = din("dftC", [128, 2, 2, 256], BF16)
    ropeC_d = din("ropeC", [128, 16, 48])
    ropeS_d = din("ropeS", [128, 16, 48])
    tri_d = din("tri", [128, 2, 128])
    mask4_d = din("mask4", [128, 2, 4, 128], BF16)
    out_d = nc.dram_tensor("outT", [D, L], F32, kind="ExternalOutput").ap()
    tap_d = {}

    P = Prog()
    with contextlib.ExitStack() as es:
        def sb(name, shape, dt):
            return es.enter_context(nc.sbuf_tensor("sb_" + name, list(shape), dt))

        xT = sb("xT", [128, 8, NTOK], F32)
        hT = sb("hT", [128, 8, NTOK], BF16)
        mixg = sb("mixg", [128, 4, NTOK], BF16)
        WSL = 3072
        wsl = [sb("wsl%d" % i, [128, WSL], BF16) for i in range(2)]
        ARENA = 54 * 1024
        arena = sb("arena", [128, ARENA // 2], BF16)
        ident = sb("ident", [128, 128], BF16)
        ones_bf = sb("ones_bf", [128, 128], BF16)
        ones_f = sb("ones_f", [128, 2], F32)
        dft64 = sb("dft64", [128, 256], BF16)
        dftC = sb("dftC", [128, 2, 2, 256], BF16)
        ropeC = sb("ropeC", [128, 16, 48], F32)
        ropeS = sb("ropeS", [128, 16, 48], F32)
        tri = sb("tri", [128, 2, 128], F32)
        mask4 = sb("mask4", [128, 2, 4, 128], BF16)
        cvec = sb("cvec", [128, 8, 2], F32)
        sc_bf = sb("sc_bf", [128, 8, 2], BF16)
        mod_sb = sb("mod_sb", [128, 24, 2], F32)
        Gm = sb("Gm", [128, 8, 2], F32)
        b_ada = sb("b_ada", [128, 24], F32)
        norm_w = sb("norm_w", [128, 8], F32)
        normf = sb("normf", [128, 8], F32)
        gnw = sb("gnw", [96, 1], F32)
        walpha = sb("walpha", [33, 2, 192], F32)
        wfour = sb("wfour", [128, 2, 256], BF16)
        pbank = [es.enter_context(nc.psum_tensor("pb%d" % i, [128, 512], F32)) for i in range(8)]
        Bp = [Buf("pb%d" % i) for i in range(8)]

        def carve(off, nelem, dt, parts=128):
            nb = nelem * (4 if dt == F32 else 2)
            assert off % 4 == 0 and off + nb <= ARENA, (off, nb, ARENA)
            v = arena[0:parts, off // 2: (off + nb) // 2]
            return v.bitcast(F32) if dt == F32 else v

        Bx = [[Buf("x%d_%d" % (c, t)) for t in range(5)] for c in range(8)]
        Bh = [[Buf("h%d_%d" % (c, t)) for t in range(5)] for c in range(8)]
        Bm = [[Buf("m%d_%d" % (c, t)) for t in range(5)] for c in range(4)]
        Bw = [Buf("wsl0"), Buf("wsl1")]
        Bc = Buf("consts")
        Bmod = Buf("mod")
        state = {"ws": 0, "pb": 0}

        def next_slot():
            k = state["ws"]
            state["ws"] = 1 - k
            return wsl[k], Bw[k]

        def next_bank():
            k = state["pb"]
            state["pb"] = (k + 1) % 8
            return pbank[k], Bp[k]

        def mm(out, lhsT, rhs, start, stop, reads, writes):
            P.op("pe", lambda e: e.matmul(out, lhsT=lhsT, rhs=rhs, start=start, stop=stop), reads, writes)

        def act(out, in_, func, reads, writes, bias=None, scale=None):
            kw = {}
            if bias is not None:
                kw["bias"] = bias
            if scale is not None:
                kw["scale"] = scale
            P.op("act", lambda e: e.activation(out=out, in_=in_, func=func, **kw), reads, writes)

        def vtt(out, in0, in1, op, reads, writes, eng="dve"):
            P.op(eng, lambda e: e.tensor_tensor(out=out, in0=in0, in1=in1, op=op), reads, writes)

        def vts(out, in0, s1, op0, reads, writes, s2=None, op1=None, eng="dve"):
            if op1 is None:
                P.op(eng, lambda e: e.tensor_scalar(out=out, in0=in0, scalar1=s1, scalar2=None, op0=op0), reads, writes)
            else:
                P.op(eng, lambda e: e.tensor_scalar(out=out, in0=in0, scalar1=s1, scalar2=s2, op0=op0, op1=op1), reads, writes)

        def vstt(out, in0, scalar, in1, op0, op1, reads, writes):
            P.op("dve", lambda e: e.scalar_tensor_tensor(out=out, in0=in0, scalar=scalar, in1=in1, op0=op0, op1=op1), reads, writes)

        def vcopy(out, in_, reads, writes, eng="dve"):
            P.op(eng, lambda e: e.tensor_copy(out=out, in_=in_), reads, writes)

        def vrecip(out, in_, reads, writes):
            P.op("dve", lambda e: e.reciprocal(out=out, in_=in_), reads, writes)

        def memset(ap, val, writes, eng="pool"):
            P.op(eng, lambda e: e.memset(ap, val), (), writes)

        def tap(name, ap, shape, reads, dt=F32):
            if name in taps:
                t = nc.dram_tensor("tap_" + name, list(shape), dt, kind="ExternalOutput").ap()
                tap_d[name] = t
                P.dma("sp", t, ap, reads=reads)

        for c in range(8):
            P.dma("sp", xT[:, c, :], xT_d[c * 128:(c + 1) * 128, :], writes=[Bx[c][t] for t in range(5)])
        for dst, src in ((ident, ident_d), (dft64, dft64_d), (dftC, dftC_d), (ropeC, ropeC_d), (ropeS, ropeS_d),
                         (tri, tri_d), (mask4, mask4_d), (normf, normf_d)):
            P.dma("sp", dst[:], src, writes=[Bc])
        P.dma("sp", cvec[:, :, :].rearrange("p c v -> p (c v)"), cvec_d, writes=[Bc])
        memset(ones_bf[:], 1.0, [Bc])
        memset(ones_f[:], 1.0, [Bc])
        act(sc_bf[:, :, :], cvec[:, :, :], AF.Silu, [Bc], [Bc])

        def load_w(l, pieces):
            slot, bw = next_slot()
            off, ntot = WIN_LAY[tuple(pieces)]
            assert 8 * ntot <= WSL
            P.dma("pool", slot[:, 0:8 * ntot], w_in_d[l][:, off:off + 8 * ntot], writes=[bw], bar=False)
            return slot[:, 0:8 * ntot].rearrange("p (c n) -> p c n", c=8), bw

        def proj_fm(wv, bw, col0, ncols, ti, ps, bps, pparts=None):
            t0, tw, _ = TBS[ti]
            for c in range(8):
                mm(ps[0:ncols, 0:tw], wv[:, c, col0:col0 + ncols], hT[:, c, t0:t0 + tw], c == 0, c == 7,
                   [bw, Bh[c][ti]], [bps])

        def proj_tm(wv, bw, col0, ncols, j, ps, bps):
            ti = min(j // 4, 4)
            for c in range(8):
                mm(ps[:, 0:ncols], hT[:, c, j * 128:(j + 1) * 128], wv[:, c, col0:col0 + ncols], c == 0, c == 7,
                   [bw, Bh[c][ti]], [bps])

        otmp = [carve(36864 + 2048 * i, 512, F32) for i in range(2)]
        Botmp = [Buf(), Buf()]

        def out_proj(l, row0, kparts, nk, srcs, tis):
            mh = 8 if nk * 1024 <= WSL else 4
            for m0 in range(0, 8, mh):
                slot, bw = next_slot()
                wv = slot[0:kparts, 0:nk * mh * 128].rearrange("p (k n) -> p k n", k=nk)
                woff = WOUT_BASE[row0] + (m0 // mh) * nk * mh * 128
                P.dma("pool", slot[0:kparts, 0:nk * mh * 128], w_out_d[l][0:kparts, woff:woff + nk * mh * 128], writes=[bw], bar=False)
                for ti in tis:
                    t0, tw, v = TBS[ti]
                    for m in range(m0, m0 + mh):
                        ps, bps = next_bank()
                        for k in range(nk):
                            mm(ps[:, 0:tw], wv[:, k, (m - m0) * 128:(m - m0 + 1) * 128], mixg[0:kparts, k, t0:t0 + tw],
                               k == 0, k == nk - 1, [bw, Bm[k][ti]], [bps])
                        if m % 2 == 0:
                            vstt(xT[:, m, t0:t0 + tw], ps[:, 0:tw], mod_sb[:, 16 + m, v:v + 1], xT[:, m, t0:t0 + tw],
                                 ALU.mult, ALU.add, [bps, Bmod, Bx[m][ti]], [Bx[m][ti]])
                        else:
                            kk_ = (m // 2) % 2
                            act(otmp[kk_][:, 0:tw], ps[:, 0:tw], AF.Identity, [bps, Bmod], [Botmp[kk_]], scale=mod_sb[:, 16 + m, v:v + 1])
                            vtt(xT[:, m, t0:t0 + tw], xT[:, m, t0:t0 + tw], otmp[kk_][:, 0:tw], ALU.add, [Botmp[kk_], Bx[m][ti]], [Bx[m][ti]],
                                eng="pool")

        for l in range(n_layers):
            last = (l == 3)
            tis_all = [0, 1, 2, 3, 4]
            tis_out = [0, 1, 2, 3] if last else tis_all

            P.phase = 'L%d.ada' % l
            P.dma("sp", b_ada[:], b_ada_d[l], writes=[Bmod])
            P.dma("sp", norm_w[:], norm_w_d[l], writes=[Bmod])
            P.dma("sp", gnw[:], gnw_d[l], writes=[Bmod])
            P.dma("sp", walpha[:, :, :].rearrange("p a b -> p (a b)"), walpha_d[l], writes=[Bmod])
            P.dma("pool", wfour[:, :, :], w_four_d[l].rearrange("(k p) n -> p k n", p=128), writes=[Bmod])
            mps, bmps = next_bank()
            mview = mps[:, 0:48].rearrange("p (j v) -> p j v", v=2)
            for pj in range(8):
                slot, bw = next_slot()
                wv = slot[:, 0:8 * 384].rearrange("p (c n) -> p c n", c=8)
                P.dma("pool", slot[:, 0:8 * 384], w_ada_d[l][:, pj * 3072:(pj + 1) * 3072], writes=[bw], bar=False)
                for j3 in range(3):
                    j = pj * 3 + j3
                    for c in range(8):
                        mm(mview[:, j, :], wv[:, c, j3 * 128:(j3 + 1) * 128], sc_bf[:, c, :], c == 0, c == 7,
                           [bw, Bc], [bmps])
            for v in range(2):
                vtt(mod_sb[:, :, v], mview[:, :, v], b_ada[:, :], ALU.add, [bmps, Bmod], [Bmod])
            for v in range(2):
                vstt(Gm[:, :, v], mod_sb[:, 8:16, v], 1.0, norm_w[:, :], ALU.add, ALU.mult, [Bmod], [Bmod])
            tap("mod%d" % l, mod_sb[:, :, :].rearrange("p j v -> p (j v)"), [128, 48], [Bmod])

            if l > 0:
                P.barrier()
            P.phase = 'L%d.norm' % l
            sqs = [carve(8192 * i, 8 * 512, BF16).rearrange("p (c n) -> p c n", c=8) for i in range(2)]
            rstds = [carve(16384 + 2048 * i, 512, F32) for i in range(2)]
            tmpn = [carve(20480 + 2048 * i, 512, F32) for i in range(4)]
            Bsqs = [[Buf() for _ in range(8)] for _ in range(2)]
            Brss, Btn = [Buf(), Buf()], [Buf() for _ in range(4)]

            def n_stat(ti):
                t0, tw, v = TBS[ti]
                sq, bsq = sqs[ti % 2], Bsqs[ti % 2]
                for c in range(8):
                    if c % 4 != 3:
                        act(sq[:, c, 0:tw], xT[:, c, t0:t0 + tw], AF.Square, [Bx[c][ti]], [bsq[c]])
                    else:
                        vtt(sq[:, c, 0:tw], xT[:, c, t0:t0 + tw], xT[:, c, t0:t0 + tw], ALU.mult, [Bx[c][ti]], [bsq[c]], eng="pool")
                ps, bps = next_bank()
                for c in range(8):
                    mm(ps[:, 0:tw], ones_bf[:, :], sq[:, c, 0:tw], c == 0, c == 7, [bsq[c], Bc], [bps])
                act(rstds[ti % 2][:, 0:tw], ps[:, 0:tw], AF.Sqrt, [bps], [Brss[ti % 2]], bias=EPS, scale=1.0 / D)
                vrecip(rstds[ti % 2][:, 0:tw], rstds[ti % 2][:, 0:tw], [Brss[ti % 2]], [Brss[ti % 2]])

            def n_mod(ti):
                t0, tw, v = TBS[ti]
                rstd, brs = rstds[ti % 2], Brss[ti % 2]
                for c in range(8):
                    k = c % 4
                    vtt(tmpn[k][:, 0:tw], xT[:, c, t0:t0 + tw], rstd[:, 0:tw], ALU.mult, [Bx[c][ti], brs], [Btn[k]],
                        eng=("dve" if c % 2 == 0 else "pool"))
                    act(hT[:, c, t0:t0 + tw], tmpn[k][:, 0:tw], AF.Identity, [Btn[k], Bmod], [Bh[c][ti]],
                        bias=mod_sb[:, c, v:v + 1], scale=Gm[:, c, v:v + 1])
            n_stat(0)
            for ti in range(5):
                if ti + 1 < 5:
                    n_stat(ti + 1)
                n_mod(ti)
            if l == 0:
                tap("hT", hT[:, :, :].rearrange("p c n -> p (c n)"), [128, 8 * NTOK], [Bh[c][t] for c in range(8) for t in range(5)], BF16)
            preA = load_w(l, ((256, 256),))
            P.barrier()

            if "A" not in skip:
                P.phase = 'L%d.A' % l
                tis = tis_out
                wv, bw = preA
                for m in range(2):
                    for ti in tis:
                        t0, tw, _ = TBS[ti]
                        ps, bps = next_bank()
                        proj_fm(wv, bw, m * 128, 128, ti, ps, bps)
                        act(mixg[:, m, t0:t0 + tw], ps[:, 0:tw], AF.Silu, [bps], [Bm[m][ti]])
                P.phase = 'L%d.A.u' % l
                uT = carve(0, 2 * NTOK, BF16).rearrange("p (m n) -> p m n", m=2)
                BuT = [[Buf() for _ in range(5)] for _ in range(2)]
                wv, bw = load_w(l, ((0, 256),))
                for m in range(2):
                    for ti in tis:
                        t0, tw, _ = TBS[ti]
                        ps, bps = next_bank()
                        proj_fm(wv, bw, m * 128, 128, ti, ps, bps)
                        vcopy(uT[:, m, t0:t0 + tw], ps[:, 0:tw], [bps], [BuT[m][ti]])
                P.phase = 'L%d.A.ucs' % l
                uCS = carve(9216, NT * 512, BF16).rearrange("p (j n) -> p j n", j=NT)
                BuCS = [Buf() for _ in range(NT)]
                njt = NT if not last else 16
                for j in range(njt):
                    ti = min(j // 4, 4)
                    ps, bps = next_bank()
                    for k in range(2):
                        mm(ps[:, k * 128:(k + 1) * 128], uT[:, k, j * 128:(j + 1) * 128], dft64[:, 0:128], True, True,
                           [BuT[k][ti], Bc], [bps])
                        mm(ps[:, 256 + k * 128:256 + (k + 1) * 128], uT[:, k, j * 128:(j + 1) * 128], dft64[:, 128:256], True, True,
                           [BuT[k][ti], Bc], [bps])
                    if j % 2 == 0:
                        act(uCS[:, j, :], ps[:, :], AF.Copy, [bps], [BuCS[j]])
                    else:
                        vcopy(uCS[:, j, :], ps[:, :], [bps], [BuCS[j]])
                P.phase = 'L%d.A.dft' % l
                dbuf = [carve(27648 + 8192 * i, 2 * L, BF16).rearrange("p (a n) -> p a n", a=2) for i in range(2)]
                Bdb = [Buf(), Buf()]
                for k in range(16):
                    db, bdb = dbuf[k % 2], Bdb[k % 2]
                    P.dma("sp", db[:, 0, :], dftL_d[0, k * 128:(k + 1) * 128, :], writes=[bdb])
                    P.dma("sp", db[:, 1, :], dftL_d[1, k * 128:(k + 1) * 128, :], writes=[bdb])
                    for m in range(2):
                        for n in range(4):
                            b = m * 4 + n
                            mm(pbank[b][:, :], uCS[:, k, m * 128:(m + 1) * 128], db[:, 0, n * 512:(n + 1) * 512], k == 0, False,
                               [BuCS[k], bdb], [Bp[b]])
                            mm(pbank[b][:, :], uCS[:, k, 256 + m * 128:256 + (m + 1) * 128], db[:, 1, n * 512:(n + 1) * 512], False, k == 15,
                               [BuCS[k], bdb], [Bp[b]])
                fT = uT
                BfT = BuT
                for m in range(2):
                    for n in range(4):
                        b = m * 4 + n
                        if b % 2 == 0:
                            act(fT[:, m, n * 512:(n + 1) * 512], pbank[b][:, :], AF.Copy, [Bp[b]] + BuCS, [BfT[m][n]])
                        else:
                            vcopy(fT[:, m, n * 512:(n + 1) * 512], pbank[b][:, :], [Bp[b]] + BuCS, [BfT[m][n]])
                if not last:
                    for m in range(2):
                        ps, bps = next_bank()
                        for k in range(2):
                            mm(ps[:, 0:256], uCS[:, 16 + k, m * 128:(m + 1) * 128], dftC[:, k, 0, :], k == 0, False,
                               [BuCS[16 + k], Bc], [bps])
                            mm(ps[:, 0:256], uCS[:, 16 + k, 256 + m * 128:256 + (m + 1) * 128], dftC[:, k, 1, :], False, k == 1,
                               [BuCS[16 + k], Bc], [bps])
                        vcopy(fT[:, m, L:NTOK], ps[:, 0:256], [bps] + BuCS, [BfT[m][4]])
                P.phase = 'L%d.A.four' % l
                for m2 in range(2):
                    for ti in tis:
                        t0, tw, _ = TBS[ti]
                        ps, bps = next_bank()
                        for m in range(2):
                            mm(ps[:, 0:tw], wfour[:, m, m2 * 128:(m2 + 1) * 128], fT[:, m, t0:t0 + tw], m == 0, m == 1,
                               [Bmod, BfT[m][ti]], [bps])
                        vtt(mixg[:, m2, t0:t0 + tw], ps[:, 0:tw], mixg[:, m2, t0:t0 + tw], ALU.mult, [bps, Bm[m2][ti]], [Bm[m2][ti]])
                if l == 0:
                    tap("mixA", mixg[:, 0:2, :].rearrange("p c n -> p (c n)"), [128, 2 * NTOK], [Bm[c][t] for c in range(2) for t in range(5)], BF16)
                P.phase = 'L%d.A.out' % l
                out_proj(l, 0, 128, 2, None, tis)
                preB = load_w(l, ((1664, 384),))
                P.barrier()

            if "B" not in skip:
                P.phase = 'L%d.B' % l
                tis = tis_out
                wv, bw = preB
                for m in range(3):
                    for ti in tis:
                        t0, tw, _ = TBS[ti]
                        ps, bps = next_bank()
                        proj_fm(wv, bw, m * 128, 128, ti, ps, bps)
                        act(mixg[:, m, t0:t0 + tw], ps[:, 0:tw], AF.Silu, [bps], [Bm[m][ti]])
                qT = carve(0, NTOK, BF16)
                kT = carve(4608, NTOK, BF16)
                vtk = carve(9216, NT * 130, BF16).rearrange("p (j h d) -> p j h d", j=NT, h=2)
                EB = carve(13896, 2 * 21 * 128, BF16).rearrange("p (h t q) -> p h t q", h=2, t=21)
                stage = carve(24648, 21 * 128, F32).rearrange("p (t q) -> p t q", t=21)
                NE = 6
                Et = [carve(35400 + 1792 * i, 7 * 128, BF16).rearrange("p (t q) -> p t q", t=7) for i in range(NE)]
                osb = [carve(46152 + 256 * i, 128, BF16) for i in range(2)]
                rden = [carve(46664 + 8 * i, 2, F32) for i in range(2)]
                Bq = [Buf() for _ in range(5)]
                Bk = [Buf() for _ in range(5)]
                Bv = [Buf() for _ in range(NT)]
                BEB, Bst = [Buf(), Buf()], Buf()
                BE = [Buf() for _ in range(NE)]
                Bos, Brd = [Buf(), Buf()], [Buf(), Buf()]
                for j in range(NT):
                    memset(vtk[:, j, :, 64:65], 1.0, [Bv[j]])
                P.phase = 'L%d.B.proj' % l
                for i in range(3):
                    wv, bw = load_w(l, ((512 + 128 * i, 128), (896 + 128 * i, 128), (1280 + 128 * i, 128),))
                    for ti in tis_all:
                        t0, tw, _ = TBS[ti]
                        if ti in tis:
                            ps, bps = next_bank()
                            proj_fm(wv, bw, 0, 128, ti, ps, bps)
                            act(qT[:, t0:t0 + tw], ps[:, 0:tw], AF.Copy, [bps], [Bq[ti]])
                        ps, bps = next_bank()
                        proj_fm(wv, bw, 128, 128, ti, ps, bps)
                        vcopy(kT[:, t0:t0 + tw], ps[:, 0:tw], [bps], [Bk[ti]])
                    for j in range(NT):
                        ps, bps = next_bank()
                        proj_tm(wv, bw, 256, 128, j, ps, bps)
                        src = ps[:, 0:128].rearrange("p (h d) -> p h d", h=2)
                        if j % 2 == 0:
                            act(vtk[:, j, :, 0:64], src, AF.Copy, [bps], [Bv[j]])
                        else:
                            vcopy(vtk[:, j, :, 0:64], src, [bps], [Bv[j]])
                    P.phase = 'L%d.B.proj' % l
                    for h2 in range(2):
                        P.dma("sp", stage[:, :, :], biasT_d[l, 2 * i + h2], writes=[Bst])
                        act(EB[:, h2, :, :], stage[:, :, :], AF.Exp, [Bst], [BEB[h2]])
                    P.phase = 'L%d.B.att' % l
                    items = []
                    for j in range(16):
                        kts, e0 = _na_keytiles(j)
                        items.append((j, kts, e0, True))
                    if not last:
                        for j in (16, 17):
                            items.append((j, [], 0, False))
                    work = [(it, h2) for it in items for h2 in range(2)]

                    def stage1(n):
                        (j, kts, e0, loc), h2 = work[n]
                        base = 64 * h2
                        E, bE = Et[n % NE], BE[n % NE]
                        allk = kts + [16, 17]
                        nl = len(kts)
                        tq = min(j // 4, 4)
                        psA, bA = next_bank()
                        psB, bB = (None, None)
                        if len(allk) > 4:
                            psB, bB = next_bank()
                        for t, kt in enumerate(allk):
                            ps, bps = (psA, bA) if t < 4 else (psB, bB)
                            tt = t % 4
                            mm(ps[:, tt * 128:(tt + 1) * 128], kT[base:base + 64, kt * 128:(kt + 1) * 128],
                               qT[base:base + 64, j * 128:(j + 1) * 128], True, True, [Bk[min(kt // 4, 4)], Bq[tq]], [bps])
                        na = min(4, len(allk))
                        act(E[:, 0:na, :], psA[:, 0:na * 128].rearrange("p (t q) -> p t q", t=na), AF.Exp, [bA], [bE], scale=0.125)
                        if len(allk) > 4:
                            nb = len(allk) - 4
                            act(E[:, 4:4 + nb, :], psB[:, 0:nb * 128].rearrange("p (t q) -> p t q", t=nb), AF.Exp, [bB], [bE], scale=0.125)
                        if nl:
                            vtt(E[:, 0:nl, :], E[:, 0:nl, :], EB[:, h2, e0:e0 + nl, :], ALU.mult, [bE, BEB[h2]], [bE])
                        return allk

                    def stage2(n, allk, ops_, bops):
                        (j, kts, e0, loc), h2 = work[n]
                        E, bE = Et[n % NE], BE[n % NE]
                        for t, kt in enumerate(allk):
                            mm(ops_[:, h2 * 128:h2 * 128 + 65], E[:, t, :], vtk[:, kt, h2, :], t == 0, t == len(allk) - 1,
                               [bE, Bv[kt]], [bops])
                        if h2 == 1:
                            pr = (n // 2) % 2
                            ov = ops_[:, 0:256].rearrange("p (h d) -> p h d", h=2)
                            vrecip(rden[pr][:, :], ov[:, :, 64], [bops], [Brd[pr]])
                            for hh in range(2):
                                vts(osb[pr][:, hh * 64:(hh + 1) * 64], ov[:, hh, 0:64], rden[pr][:, hh:hh + 1], ALU.mult,
                                    [bops, Brd[pr]], [Bos[pr]])
                            pt, bpt = next_bank()
                            ptb = pt[:, 0:64].bitcast(BF16)
                            P.op("pe", lambda e: e.transpose(out=ptb, in_=osb[pr][:, :], identity=ident[:, :]), [Bos[pr], Bc], [bpt])
                            tq = min(j // 4, 4)
                            vtt(mixg[:, i, j * 128:(j + 1) * 128], ptb, mixg[:, i, j * 128:(j + 1) * 128], ALU.mult,
                                [bpt, Bm[i][tq]], [Bm[i][tq]])

                    DEPTH = 2
                    allks = {}
                    cur_o = None
                    for n in range(len(work) + DEPTH):
                        if n < len(work):
                            allks[n] = stage1(n)
                        pn = n - DEPTH
                        if pn >= 0:
                            if work[pn][1] == 0:
                                cur_o = next_bank()
                            stage2(pn, allks.pop(pn), cur_o[0], cur_o[1])
                if l == 0:
                    tap("mixB", mixg[:, 0:3, :].rearrange("p c n -> p (c n)"), [128, 3 * NTOK], [Bm[c][t] for c in range(3) for t in range(5)], BF16)
                P.phase = 'L%d.B.out' % l
                out_proj(l, 256, 128, 3, None, tis)
                preC = load_w(l, ((2816, 384),))
                P.barrier()

            if "C" not in skip:
                P.phase = 'L%d.C' % l
                tis = tis_out
                wv, bw = preC
                for h in range(4):
                    for ti in tis:
                        t0, tw, _ = TBS[ti]
                        ps, bps = next_bank()
                        proj_fm(wv, bw, h * 96, 96, ti, ps, bps)
                        act(mixg[0:96, h, t0:t0 + tw], ps[0:96, 0:tw], AF.Silu, [bps], [Bm[h][ti]])
                qkr = carve(0, NT * 384, BF16).rearrange("p (j g d) -> p j g d", j=NT, g=8)
                vg = carve(13824, NT * 384, BF16).rearrange("p (j n) -> p j n", j=NT)
                zT = carve(27648, NTOK, F32, parts=33)
                off = 36864
                Bqk = [Buf() for _ in range(NT)]
                Bvg = [Buf() for _ in range(NT)]
                Bz = [Buf() for _ in range(5)]

                def cv(n, dt, parts=128):
                    nonlocal off
                    v = carve(off, n, dt, parts)
                    off += n * (4 if dt == F32 else 2)
                    off = (off + 3) // 4 * 4
                    return v
                t1 = cv(384, F32)
                t2 = cv(384, F32)
                Bt1, Bt2 = Buf(), Buf()
                P.phase = 'L%d.C.proj' % l
                wv, bw = load_w(l, ((2048, 384),))
                for j in range(NT):
                    ps, bps = next_bank()
                    proj_tm(wv, bw, 0, 384, j, ps, bps)
                    if j < 16:
                        cb = ropeC[:, j, :].unsqueeze(1).to_broadcast([128, 8, 48])
                        vtt(t1[:, :].rearrange("p (g d) -> p g d", g=8), ps[:, 0:384].rearrange("p (g d) -> p g d", g=8), cb,
                            ALU.mult, [bps, Bc], [Bt1])
                        psv = ps[:, 0:384].rearrange("p (g f u w) -> p g f u w", g=8, f=2, u=2)
                        t2v = t2[:, :].rearrange("p (g f u w) -> p g f u w", g=8, f=2, u=2)
                        sv = ropeS[:, j, :].rearrange("p (f u w) -> p f u w", f=2, u=2)
                        for u in range(2):
                            vtt(t2v[:, :, :, u, :], psv[:, :, :, 1 - u, :], sv[:, :, u, :].unsqueeze(1).to_broadcast([128, 8, 2, 12]),
                                ALU.mult, [bps, Bc], [Bt2])
                        vtt(qkr[:, j, :, :], t1[:, :].rearrange("p (g d) -> p g d", g=8), t2[:, :].rearrange("p (g d) -> p g d", g=8),
                            ALU.add, [Bt1, Bt2], [Bqk[j]])
                    else:
                        vcopy(qkr[:, j, :, :], ps[:, 0:384].rearrange("p (g d) -> p g d", g=8), [bps], [Bqk[j]])
                wv, bw = load_w(l, ((2432, 384),))
                for j in range(NT):
                    ps, bps = next_bank()
                    proj_tm(wv, bw, 0, 384, j, ps, bps)
                    if j % 2 == 0:
                        act(vg[:, j, :], ps[:, 0:384], AF.Copy, [bps], [Bvg[j]])
                    else:
                        vcopy(vg[:, j, :], ps[:, 0:384], [bps], [Bvg[j]])
                wv, bw = load_w(l, ((3200, 32),))
                memset(zT[32:33, :], 1.0, Bz)
                for ti in tis_all:
                    t0, tw, _ = TBS[ti]
                    ps, bps = next_bank()
                    proj_fm(wv, bw, 0, 32, ti, ps, bps)
                    vcopy(zT[0:32, t0:t0 + tw], ps[0:32, 0:tw], [bps], [Bz[ti]])
                P.barrier()
                oacc = hT[0:96, :, :].rearrange("p c n -> p (c n)").bitcast(F32).rearrange("p (h n) -> p h n", h=4)
                Boa = [Buf() for _ in range(NT)]
                class _Al:
                    def __init__(self, ap2d, nbytes, start=0):
                        self.ap, self.n, self.off = ap2d, nbytes, start

                    def __call__(self, nelem, dt, parts=128):
                        nb = nelem * (4 if dt == F32 else 2)
                        assert self.off % 4 == 0 and self.off + nb <= self.n, (self.off, nb, self.n)
                        v = self.ap[0:parts, self.off // 2:(self.off + nb) // 2]
                        self.off = (self.off + nb + 3) // 4 * 4
                        return v.bitcast(F32) if dt == F32 else v
                tot = carve(36864, 512, F32, 96)
                sqo = carve(36864 + 2048, 512, BF16, 96)
                al0 = _Al(arena, ARENA, 39936)
                alA, alB = _Al(wsl[0], 2 * WSL), _Al(wsl[1], 2 * WSL)
                Btot, Bsqo = Buf(), Buf()
                LNS = float(np.log(48.0 ** -0.5))
                SC = []
                for d in range(2):
                    a1, a2 = (al0, al0) if d == 0 else (alA, alB)
                    sc = {}
                    sc["spd"] = [a1(256, F32) for _ in range(2)]
                    sc["eq"] = [a1(192, F32)]
                    sc["ek"] = [a1(192, F32)]
                    sc["qd"] = [a1(256, BF16) for _ in range(2)]
                    sc["kd"] = [a1(256, BF16) for _ in range(3)]
                    sc["qkT"] = [a2(512, BF16) for _ in range(2)]
                    sc["Am"] = [a2(512, BF16) for _ in range(2)]
                    sc["ebend"] = [a2(2, F32) for _ in range(3)]
                    sc["st_f"] = a2(192, F32)
                    sc["st_t"] = a2(192, F32)
                    sc["st_b"] = a2(192, BF16)
                    sc["B"] = {k: [Buf() for _ in v] for k, v in sc.items() if isinstance(v, list)}
                    sc["Bst_f"], sc["Bst_t"], sc["Bst_b"] = Buf(), Buf(), Buf()
                    for k in ("spd", "qd", "kd"):
                        for i_, t_ in enumerate(sc[k]):
                            memset(t_[:, :], 0.0, [sc["B"][k][i_]])
                    memset(sc["st_f"][:, :], 0.0, [sc["Bst_f"]])
                    memset(sc["st_b"][:, :], 0.0, [sc["Bst_b"]])
                    sc["order"] = [16, 17] + list(range(16)) if d == 0 else [17, 16] + list(range(15, -1, -1))
                    SC.append(sc)
                P.phase = 'L%d.C.sweep' % l
                stored = set()

                def h4(ap):
                    return ap.rearrange("p (h d) -> p h d", h=4)

                def st_a(d, n):
                    sc = SC[d]; c = sc["order"][n]; ti = min(c // 4, 4)
                    spd, bspd = sc["spd"][n % 2], sc["B"]["spd"][n % 2]
                    gps, bg = next_bank()
                    mm(gps[:, 0:192], zT[0:33, c * 128:(c + 1) * 128], walpha[0:33, d, :], True, True, [Bz[ti], Bmod], [bg])
                    spv = h4(spd[:, :])[:, :, 0:48]
                    act(spv, h4(gps[:, 0:192]), AF.Exp, [bg], [bspd], scale=-1.0)
                    act(spv, spv, AF.Ln, [bspd], [bspd], bias=1.0)

                def st_b(d, n):
                    sc = SC[d]; c = sc["order"][n]
                    spd, bspd = sc["spd"][n % 2], sc["B"]["spd"][n % 2]
                    eq, beq = sc["eq"][0], sc["B"]["eq"][0]
                    ek, bek = sc["ek"][0], sc["B"]["ek"][0]
                    qd, bqd = sc["qd"][n % 2], sc["B"]["qd"][n % 2]
                    kd, bkd = sc["kd"][n % 3], sc["B"]["kd"][n % 3]
                    eb, beb = sc["ebend"][n % 3], sc["B"]["ebend"][n % 3]
                    bps_, bb = next_bank()
                    mm(bps_[:, 0:256], tri[:, d, :], spd[:, :], True, True, [Bc, bspd], [bb])
                    for i in range(2):
                        mm(bps_[:, 256 + 2 * i:258 + 2 * i], spd[:, i * 128:(i + 1) * 128], ones_f[:, 0:2], True, True, [bspd, Bc], [bb])
                    bv = h4(bps_[:, 0:256])[:, :, 0:48]
                    act(h4(eq[:, :]), bv, AF.Exp, [bb], [beq], bias=LNS, scale=-1.0 / 16)
                    act(h4(ek[:, :]), bv, AF.Exp, [bb], [bek], scale=1.0 / 16)
                    act(eb[:, :], bps_[:, 256:260].rearrange("p (i two) -> p i two", two=2)[:, :, 0], AF.Exp, [bb], [beb], scale=-1.0 / 16)
                    vtt(h4(qd[:, :])[:, :, 0:48], qkr[:, c, 0:4, :], h4(eq[:, :]), ALU.mult, [Bqk[c], beq], [bqd])
                    vtt(h4(kd[:, :])[:, :, 0:48], qkr[:, c, 4:8, :], h4(ek[:, :]), ALU.mult, [Bqk[c], bek], [bkd])

                def st_c(d, n):
                    sc = SC[d]
                    qd, bqd = sc["qd"][n % 2], sc["B"]["qd"][n % 2]
                    kd, bkd = sc["kd"][n % 3], sc["B"]["kd"][n % 3]
                    qkT, bqkT = sc["qkT"][n % 2], sc["B"]["qkT"][n % 2]
                    tp, btp = next_bank()
                    tpb = tp[:, 0:256].bitcast(BF16)
                    for i in range(2):
                        P.op("pe", lambda e, a=tpb[:, i * 128:(i + 1) * 128], b=qd[:, i * 128:(i + 1) * 128]:
                             e.transpose(out=a, in_=b, identity=ident[:, :]), [bqd, Bc], [btp])
                        P.op("pe", lambda e, a=tpb[:, (2 + i) * 128:(3 + i) * 128], b=kd[:, i * 128:(i + 1) * 128]:
                             e.transpose(out=a, in_=b, identity=ident[:, :]), [bkd, Bc], [btp])
                    act(qkT[:, :], tpb, AF.Copy, [btp], [bqkT])

                def st_d(d, n):
                    sc = SC[d]
                    qkT, bqkT = sc["qkT"][n % 2], sc["B"]["qkT"][n % 2]
                    Am, bAm = sc["Am"][n % 2], sc["B"]["Am"][n % 2]
                    apsb = [next_bank(), next_bank()]
                    Amv = Am[:, :].rearrange("p (i b t) -> p i b t", i=2, b=2)
                    for b in range(2):
                        aps, ba = apsb[b]
                        base = 64 * b
                        for i in range(2):
                            mm(aps[:, i * 128:(i + 1) * 128], qkT[base:base + 64, (2 + i) * 128:(3 + i) * 128],
                               qkT[base:base + 64, i * 128:(i + 1) * 128], True, True, [bqkT], [ba])
                    for b in range(2):
                        aps, ba = apsb[b]
                        vtt(Amv[:, :, b, :], aps[:, 0:256].rearrange("p (i t) -> p i t", i=2), mask4[:, d, 0:2, :], ALU.mult,
                            [ba, Bc], [bAm])

                def st_e(d, n):
                    sc = SC[d]; c = sc["order"][n]; ti = min(c // 4, 4)
                    kd, bkd = sc["kd"][n % 3], sc["B"]["kd"][n % 3]
                    qkT, bqkT = sc["qkT"][n % 2], sc["B"]["qkT"][n % 2]
                    Am, bAm = sc["Am"][n % 2], sc["B"]["Am"][n % 2]
                    eb, beb = sc["ebend"][n % 3], sc["B"]["ebend"][n % 3]
                    st_f, st_t, stb = sc["st_f"], sc["st_t"], sc["st_b"]
                    opsb = [next_bank(), next_bank()]
                    for b in range(2):
                        ops_, bo = opsb[b]
                        base = 64 * b
                        for i in range(2):
                            h = 2 * i + b
                            mm(ops_[0:96, i * 128:(i + 1) * 128], vg[:, c, h * 96:(h + 1) * 96], Am[:, h * 128:(h + 1) * 128], True, False,
                               [Bvg[c], bAm], [bo])
                            mm(ops_[0:96, i * 128:(i + 1) * 128], stb[base:base + 64, i * 96:(i + 1) * 96],
                               qkT[base:base + 64, i * 128:(i + 1) * 128], False, True, [sc["Bst_b"], bqkT], [bo])
                    ups, bu = next_bank()
                    for h in range(4):
                        i, base = h // 2, 64 * (h % 2)
                        mm(ups[base:base + 64, i * 96:(i + 1) * 96], kd[:, h * 64:(h + 1) * 64], vg[:, c, h * 96:(h + 1) * 96],
                           True, True, [bkd, Bvg[c]], [bu])
                    vtt(st_t[:, :], ups[:, 0:192], st_f[:, :], ALU.add, [bu, sc["Bst_f"]], [sc["Bst_t"]])
                    for i in range(2):
                        act(stb[:, i * 96:(i + 1) * 96], st_t[:, i * 96:(i + 1) * 96], AF.Identity, [sc["Bst_t"], beb], [sc["Bst_b"]],
                            scale=eb[:, i:i + 1])
                    for i in range(2):
                        vts(st_f[:, i * 96:(i + 1) * 96], st_t[:, i * 96:(i + 1) * 96], eb[:, i:i + 1], ALU.mult,
                            [sc["Bst_t"], beb], [sc["Bst_f"]])
                    oav = oacc[:, :, c * 128:(c + 1) * 128].rearrange("p (i b) t -> p i b t", b=2)
                    totv = tot[:, :].rearrange("p (i b t) -> p i b t", i=2, b=2)
                    if c not in stored:
                        stored.add(c)
                        for b in range(2):
                            ops_, bo = opsb[b]
                            ov = ops_[0:96, 0:256].rearrange("p (i t) -> p i t", i=2)
                            if b == 0:
                                act(oav[:, :, b, :], ov, AF.Copy, [bo], [Boa[c]])
                            else:
                                vcopy(oav[:, :, b, :], ov, [bo], [Boa[c]])
                    elif not (last and c >= 16):
                        for b in range(2):
                            ops_, bo = opsb[b]
                            ov = ops_[0:96, 0:256].rearrange("p (i t) -> p i t", i=2)
                            vtt(totv[:, :, b, :], ov, oav[:, :, b, :], ALU.add, [bo, Boa[c]], [Btot])
                        act(sqo[:, :], tot[:, :], AF.Square, [Btot], [Bsqo])
                        sps, bs_ = next_bank()
                        mm(sps[0:96, :], ones_bf[0:96, 0:96], sqo[:, :], True, True, [Bsqo, Bc], [bs_])
                        act(sps[0:96, :], sps[0:96, :], AF.Sqrt, [bs_], [bs_], bias=EPS, scale=1.0 / 96)
                        vrecip(sps[0:96, :], sps[0:96, :], [bs_], [bs_])
                        vtt(tot[:, :], tot[:, :], sps[0:96, :], ALU.mult, [Btot, bs_], [Btot])
                        mv = mixg[0:96, :, c * 128:(c + 1) * 128]
                        vstt(mv, tot[:, :].rearrange("p (h t) -> p h t", h=4), gnw[:, 0:1], mv, ALU.mult, ALU.mult,
                             [Btot, Bmod] + [Bm[h][ti] for h in range(4)], [Bm[h][ti] for h in range(4)])

                def ok(n):
                    return 0 <= n < NT
                for t in range(-3, NT):
                    for fn, dn in ((st_a, 3), (st_b, 2), (st_c, 1), (st_e, 0), (st_d, 1)):
                        for d in range(2):
                            if ok(t + dn) and ('C' + fn.__name__[-1]) not in skip:
                                fn(d, t + dn)
                P.barrier()
                if l == 0:
                    tap("mixC", mixg[0:96, :, :].rearrange("p c n -> p (c n)"), [96, 4 * NTOK], [Bm[c][t] for c in range(4) for t in range(5)], BF16)
                P.phase = 'L%d.C.out' % l
                out_proj(l, 640, 96, 4, None, tis)
                if l == n_layers - 1:
                    P.barrier()
            if l == 0:
                tap("x1", xT[:, :, :].rearrange("p c n -> p (c n)"), [128, 8 * NTOK], [Bx[c][t] for c in range(8) for t in range(5)])

        P.phase = 'final'
        sq = carve(0, 8 * 512, BF16).rearrange("p (c n) -> p c n", c=8)
        rstd = carve(8192, 512, F32)
        ost = [carve(10240 + 2048 * i, 512, F32) for i in range(4)]
        Bsq, Brs, Bos_ = Buf("sq"), Buf("rstd"), [Buf() for _ in range(4)]
        for ti in range(4):
            t0, tw, v = TBS[ti]
            for c in range(8):
                act(sq[:, c, 0:tw], xT[:, c, t0:t0 + tw], AF.Square, [Bx[c][ti]], [Bsq])
            ps, bps = next_bank()
            for c in range(8):
                mm(ps[:, 0:tw], ones_bf[:, :], sq[:, c, 0:tw], c == 0, c == 7, [Bsq, Bc], [bps])
            act(rstd[:, 0:tw], ps[:, 0:tw], AF.Sqrt, [bps], [Brs], bias=EPS, scale=1.0 / D)
            vrecip(rstd[:, 0:tw], rstd[:, 0:tw], [Brs], [Brs])
            for c in range(8):
                k = c % 4
                vstt(ost[k][:, 0:tw], xT[:, c, t0:t0 + tw], normf[:, c:c + 1], rstd[:, 0:tw], ALU.mult, ALU.mult,
                     [Bx[c][ti], Brs, Bc], [Bos_[k]])
                P.dma("sp", out_d[c * 128:(c + 1) * 128, t0:t0 + tw], ost[k][:, 0:tw], reads=[Bos_[k]])
        stats = P.emit(nc)
    _NC_CACHE['prog'] = P
    return nc, stats, tap_d


_NC_CACHE = {}


def prep_inputs(x, c, ctx, c_ctx, w_ada, b_ada, norm_w, w_in, w_four, rpb, w_alpha_fwd, b_alpha_fwd,
                w_alpha_bwd, b_alpha_bwd, gla_norm_w, w_out, norm_f, cores):
    C = _consts()
    f32 = np.float32
    x = np.asarray(x, f32); ctx = np.asarray(ctx, f32); c = np.asarray(c, f32); c_ctx = np.asarray(c_ctx, f32)
    rpb = np.asarray(rpb, f32)
    nl = rpb.shape[0]
    rpb_pad = np.concatenate([rpb.reshape(nl, 6, -1), np.full((nl, 6, 1), PAD_BIAS, f32)], axis=-1)
    biasT = rpb_pad[:, :, C["naidx"]]
    biasT = np.ascontiguousarray(biasT.transpose(0, 1, 3, 2, 4))
    walpha = np.zeros((nl, 33, 2, 192), f32)
    walpha[:, 0:16, 0, :] = np.asarray(w_alpha_fwd, f32)
    walpha[:, 16:32, 1, :] = np.asarray(w_alpha_bwd, f32)
    walpha[:, 32, 0, :] = np.asarray(b_alpha_fwd, f32)
    walpha[:, 32, 1, :] = np.asarray(b_alpha_bwd, f32)
    w_in = np.asarray(w_in, f32); w_ada = np.asarray(w_ada, f32); w_out = np.asarray(w_out, f32)
    pieces, _lay = _win_pieces()
    w_in_r = np.empty((nl, 128, 8 * DIN), f32)
    w_ada_r = np.empty((nl, 128, 8 * 3 * D), f32)
    w_out_r = np.zeros((nl, 128, 9216), f32)
    for l_ in range(nl):
        Wl = w_in[l_].reshape(8, 128, DIN).transpose(1, 0, 2)
        o_ = 0
        for p_ in pieces:
            blk = np.concatenate([Wl[:, :, c0:c0 + k] for c0, k in p_], axis=2)
            w_in_r[l_, :, o_:o_ + blk.shape[1] * blk.shape[2]] = blk.reshape(128, -1)
            o_ += blk.shape[1] * blk.shape[2]
        Wa = w_ada[l_].reshape(8, 128, 3 * D).transpose(1, 0, 2)
        for pj in range(8):
            w_ada_r[l_, :, pj * 3072:(pj + 1) * 3072] = Wa[:, :, pj * 384:(pj + 1) * 384].reshape(128, 3072)
        w_out_r[l_, :, 0:2048] = w_out[l_, 0:256].reshape(2, 128, D).transpose(1, 0, 2).reshape(128, 2048)
        w_out_r[l_, :, 2048:5120] = w_out[l_, 256:640].reshape(3, 128, D).transpose(1, 0, 2).reshape(128, 3072)
        Wc = w_out[l_, 640:1024].reshape(4, 96, D).transpose(1, 0, 2)
        w_out_r[l_, 0:96, 5120:7168] = Wc[:, :, 0:512].reshape(96, 2048)
        w_out_r[l_, 0:96, 7168:9216] = Wc[:, :, 512:1024].reshape(96, 2048)
    shared = {
        "w_ada_r": w_ada_r,
        "b_ada": np.ascontiguousarray(np.asarray(b_ada, f32).reshape(nl, 24, 128).transpose(0, 2, 1)),
        "norm_w": np.ascontiguousarray(np.asarray(norm_w, f32).reshape(nl, 8, 128).transpose(0, 2, 1)),
        "w_in_r": w_in_r,
        "w_four": np.ascontiguousarray(np.asarray(w_four, f32)),
        "biasT": biasT,
        "walpha": walpha.reshape(nl, 33, 384),
        "gnw": np.ascontiguousarray(np.asarray(gla_norm_w, f32).reshape(nl, 96, 1)),
        "w_out_r": w_out_r,
        "normf": np.ascontiguousarray(np.asarray(norm_f, f32).reshape(8, 128).T),
        "ident": C["ident"], "dft64": C["dft64"], "dftL": C["dftL"], "dftC": C["dftC"],
        "ropeC": C["ropeC"], "ropeS": C["ropeS"], "tri": C["tri"], "mask4": C["mask4"],
    }
    in_maps = []
    for b in cores:
        xT = np.ascontiguousarray(np.concatenate([x[b], ctx[b]], axis=0).T)
        cv = np.stack([c[b].reshape(8, 128).T, c_ctx.reshape(8, 128).T], axis=-1)
        m = dict(shared)
        m["xT"] = xT
        m["cvec"] = np.ascontiguousarray(cv.reshape(128, 16))
        in_maps.append(m)
    return in_maps


def kernel(**inputs):
    if "nc" not in _NC_CACHE:
        _NC_CACHE["nc"] = build()[0]
    nc = _NC_CACHE["nc"]
    in_maps = prep_inputs(cores=list(range(8)), **inputs)
    res = run_bass_kernel_spmd(nc, in_maps, core_ids=list(range(8)))
    out = np.stack([np.asarray(r["outT"]).T for r in res.results], axis=0)
    return np.ascontiguousarray(out.astype(np.float32))
```

```python
import contextlib
import numpy as np
import ml_dtypes
import concourse.bass as bass
import concourse.mybir as mybir
from concourse.bass_utils import run_bass_kernel_spmd

F32 = mybir.dt.float32
BF16 = mybir.dt.bfloat16
AF = mybir.ActivationFunctionType
ALU = mybir.AluOpType
NPBF = ml_dtypes.bfloat16

D = 1024
L = 2048
LC = 256
NTOK = L + LC
NT = NTOK // 128
DIN = 3232
EPS = 1e-6
PAD_BIAS = -100.0
TBS = [(0, 512, 0), (512, 512, 0), (1024, 512, 0), (1536, 512, 0), (2048, 256, 1)]

ENGS = ("sp", "pool", "act", "dve", "pe")


class Buf:
    __slots__ = ("name", "writer", "readers")

    def __init__(self, name=""):
        self.name = name
        self.writer = None
        self.readers = []


class Op:
    __slots__ = ("eng", "fn", "deps", "signal", "sem", "val", "is_dma", "waits", "bar", "phase")

    def __init__(self, eng, fn, is_dma=False, bar=True):
        self.eng = eng
        self.fn = fn
        self.deps = []
        self.signal = False
        self.sem = None
        self.val = 0
        self.is_dma = is_dma
        self.waits = []
        self.bar = bar


class Prog:
    def __init__(self):
        self.ops = []
        self.last = {}
        self.pend = {e: [] for e in ENGS}
        self.bar_dmas = []
        self.phase = ''

    def op(self, eng, fn, reads=(), writes=(), is_dma=False, bar=True):
        o = Op(eng, fn, is_dma, bar)
        o.phase = self.phase
        deps = []
        for b in reads:
            if b.writer is not None:
                deps.append(b.writer)
        for b in writes:
            if b.writer is not None:
                deps.append(b.writer)
            deps.extend(b.readers)
        if self.pend[eng]:
            deps.extend(self.pend[eng])
            self.pend[eng] = []
        seen = set()
        for d in deps:
            if d is o or id(d) in seen:
                continue
            seen.add(id(d))
            if d.eng == "pe" and eng == "pe" and not d.is_dma and not is_dma:
                continue
            o.deps.append(d)
            d.signal = True
        for b in reads:
            if not is_dma:
                b.readers = [r for r in b.readers if r.is_dma or r.eng != eng]
            b.readers.append(o)
        for b in writes:
            b.writer = o
            b.readers = []
        self.ops.append(o)
        if is_dma:
            if bar:
                self.bar_dmas.append(o)
        else:
            self.last[eng] = o
        return o

    def dma(self, q, out, in_, reads=(), writes=(), bar=True):
        return self.op(q, lambda e: e.dma_start(out=out, in_=in_), reads, writes, is_dma=True, bar=bar)

    def barrier(self):
        deps = list(self.last.values()) + self.bar_dmas
        self.bar_dmas = []
        for e in ENGS:
            self.pend[e] = self.pend[e] + deps

    def emit(self, nc, n_hw_sems=16, n_sw_sems=32):
        ops = self.ops
        with contextlib.ExitStack() as es:
            esem = {e: es.enter_context(nc.semaphore("s_" + e)) for e in ENGS}
            hsems = [es.enter_context(nc.semaphore("h%d" % i)) for i in range(n_hw_sems)]
            ssems = [es.enter_context(nc.semaphore("w%d" % i)) for i in range(n_sw_sems)]
            block = es.enter_context(nc.Block())
            pos = {id(o): i for i, o in enumerate(ops)}
            cons = {}
            for o in ops:
                for d in o.deps:
                    if d.is_dma and d.eng == "pool":
                        cons.setdefault(id(d), []).append(o)
            hcount = [0] * n_hw_sems
            hlast = [None] * n_hw_sems
            slast = [None] * n_sw_sems
            sgen = [0] * n_sw_sems
            hi = si = 0
            clear_before = {}
            gen_of = {}
            all_dma = []
            for o in ops:
                if not o.is_dma:
                    continue
                o.signal = True
                all_dma.append(o)
                if o.eng == "pool":
                    k = si % n_sw_sems
                    si += 1
                    prev = slast[k]
                    if prev is not None and all(d is not prev for d in o.deps):
                        o.deps.append(prev)
                    sgen[k] += 16
                    o.sem = ssems[k]
                    o.val = sgen[k]
                    slast[k] = o
                else:
                    k = hi % n_hw_sems
                    hi += 1
                    if hlast[k] is not None and all(d is not hlast[k] for d in o.deps):
                        o.deps.append(hlast[k])
                    hcount[k] += 16
                    o.sem = hsems[k]
                    o.val = hcount[k]
                    hlast[k] = o
            cnt = {e: 0 for e in ENGS}
            for o in ops:
                if o.signal and not o.is_dma:
                    cnt[o.eng] += 1
                    o.sem = esem[o.eng]
                    o.val = cnt[o.eng]
            seen = {e: {} for e in ENGS}
            for o in ops:
                need = {}
                for d in o.deps:
                    key = id(d.sem)
                    if key not in need or need[key][1] < d.val:
                        need[key] = (d.sem, d.val)
                for key, (sem, val) in need.items():
                    if seen[o.eng].get(key, 0) >= val:
                        continue
                    seen[o.eng][key] = val
                    o.waits.append((sem, val))
            tail = {}
            for o in all_dma:
                tail[id(o.sem)] = (o.sem, o.val)

            def run(engname, eng):
                for o in ops:
                    if o.eng != engname:
                        continue
                    for sem, val in o.waits:
                        eng.wait_ge(sem, val)
                    if id(o) in clear_before:
                        eng.sem_clear(clear_before[id(o)])
                    ins = o.fn(eng)
                    if o.signal:
                        ins.then_inc(o.sem, 16 if o.is_dma else 1)
                if engname == "sp":
                    for sem, val in tail.values():
                        eng.wait_ge(sem, val)

            @block.sync
            def _(e):
                run("sp", e)

            @block.gpsimd
            def _(e):
                run("pool", e)

            @block.scalar
            def _(e):
                run("act", e)

            @block.vector
            def _(e):
                run("dve", e)

            @block.tensor
            def _(e):
                run("pe", e)
        return {e: sum(1 for o in ops if o.eng == e) for e in ENGS}, dict(cnt)


def _na_tiles():
    return ([(5, 5 + dk) for dk in (-2, -1, 0, 1, 2)] + [(0, k) for k in range(4)] + [(1, k) for k in range(4)]
            + [(14, k) for k in (12, 13, 14, 15)] + [(15, k) for k in (12, 13, 14, 15)])


def _na_keytiles(j):
    if j == 0:
        return [0, 1, 2, 3], 5
    if j == 1:
        return [0, 1, 2, 3], 9
    if j == 14:
        return [12, 13, 14, 15], 13
    if j == 15:
        return [12, 13, 14, 15], 17
    return [j - 2, j - 1, j, j + 1, j + 2], 0


def _na_bias_index():
    rows, kh = 32, 8
    rs = np.clip(np.arange(rows) - 4, 0, rows - kh)
    cs = np.clip(np.arange(64) - 8, 0, 48)
    tiles = _na_tiles()
    pad = 15 * 31
    idx = np.full((len(tiles), 128, 128), pad, np.int64)
    kc = np.arange(64)[:, None]
    qc = np.arange(64)[None, :]
    co = np.clip(kc - qc + 15, 0, 30)
    valid = (kc >= cs[qc]) & (kc < cs[qc] + 16)
    for t, (j, kt) in enumerate(tiles):
        for a in range(2):
            kr = 2 * kt + a
            for c in range(2):
                r = 2 * j + c
                if not (rs[r] <= kr < rs[r] + kh):
                    continue
                ro = kr - r + 7
                idx[t, a * 64:(a + 1) * 64, c * 64:(c + 1) * 64] = np.where(valid, ro * 31 + co, pad)
    return idx


def _win_pieces():
    ps = [((256, 256),), ((0, 256),), ((1664, 384),)]
    ps += [((512 + 128 * i, 128), (896 + 128 * i, 128), (1280 + 128 * i, 128)) for i in range(3)]
    ps += [((2816, 384),), ((2048, 384),), ((2432, 384),), ((3200, 32),)]
    lay, off = {}, 0
    for p in ps:
        n = sum(k for _, k in p)
        lay[p] = (off, n)
        off += 8 * n
    assert off == 8 * DIN
    return ps, lay


WOUT_BASE = {0: 0, 256: 2048, 640: 5120}

_CONST_CACHE = {}


def _consts():
    if _CONST_CACHE:
        return _CONST_CACHE
    C = _CONST_CACHE
    C["ident"] = np.eye(128, dtype=np.float32).astype(NPBF)
    k = np.arange(64)
    ang = 2 * np.pi * np.outer(k, k) / 64.0
    c64 = np.cos(ang) / 8.0
    s64 = -np.sin(ang) / 8.0
    blk = np.zeros((128, 256), np.float64)
    for g in range(2):
        blk[g * 64:(g + 1) * 64, g * 64:(g + 1) * 64] = c64
        blk[g * 64:(g + 1) * 64, 128 + g * 64:128 + (g + 1) * 64] = s64
    C["dft64"] = blk.astype(np.float32).astype(NPBF)

    def dftn(n):
        t = np.arange(n, dtype=np.int64)
        m = np.outer(t, t) % n
        a = 2 * np.pi * m.astype(np.float64) / n
        return np.stack([np.cos(a), np.sin(a)]).astype(np.float32) / np.float32(np.sqrt(n))
    C["dftL"] = dftn(L).astype(NPBF)
    dc = dftn(LC).astype(NPBF)
    C["dftC"] = np.ascontiguousarray(dc.reshape(2, 2, 128, 256).transpose(2, 1, 0, 3))
    pos = np.arange(L)
    row = (pos // 64).astype(np.float32)
    col = (pos % 64).astype(np.float32)
    inv = (10000.0 ** (-np.arange(12, dtype=np.float32) / 12)).astype(np.float32)
    ar = row[:, None] * inv[None, :]
    ac = col[:, None] * inv[None, :]
    cos48 = np.concatenate([np.cos(ar), np.cos(ar), np.cos(ac), np.cos(ac)], -1)
    sin48 = np.concatenate([-np.sin(ar), np.sin(ar), -np.sin(ac), np.sin(ac)], -1)
    C["ropeC"] = np.ascontiguousarray(cos48.reshape(16, 128, 48).transpose(1, 0, 2)).astype(np.float32)
    C["ropeS"] = np.ascontiguousarray(sin48.reshape(16, 128, 48).transpose(1, 0, 2)).astype(np.float32)
    s = np.arange(128)[:, None]
    t = np.arange(128)[None, :]
    tri = np.stack([(s <= t), (s >= t)]).astype(np.float32)
    C["tri"] = np.ascontiguousarray(tri.transpose(1, 0, 2))
    m4 = np.repeat(tri[:, :, None, :], 4, axis=2)
    C["mask4"] = np.ascontiguousarray(m4.transpose(1, 0, 2, 3)).astype(NPBF)
    C["naidx"] = _na_bias_index()
    return C


def build(n_layers=4, taps=(), skip=()):
    nc = bass.Bass("TRN2", target_bir_lowering=False)

    def din(name, shape, dt=F32):
        return nc.dram_tensor(name, list(shape), dt, kind="ExternalInput").ap()

    xT_d = din("xT", [D, NTOK])
    cvec_d = din("cvec", [128, 16])
    w_ada_d = din("w_ada_r", [4, 128, 8 * 3 * D])
    b_ada_d = din("b_ada", [4, 128, 24])
    norm_w_d = din("norm_w", [4, 128, 8])
    w_in_d = din("w_in_r", [4, 128, 8 * DIN])
    WIN_LAY = _win_pieces()[1]
    w_four_d = din("w_four", [4, 256, 256])
    biasT_d = din("biasT", [4, 6, 128, 21, 128])
    walpha_d = din("walpha", [4, 33, 384])
    gnw_d = din("gnw", [4, 96, 1])
    w_out_d = din("w_out_r", [4, 128, 9216])
    normf_d = din("normf", [128, 8])
    ident_d = din("ident", [128, 128], BF16)
    dft64_d = din("dft64", [128, 256], BF16)
    dftL_d = din("dftL", [2, L, L], BF16)
    dftC_d = din("dftC", [128, 2, 2, 256], BF16)
    ropeC_d = din("ropeC", [128, 16, 48])
    ropeS_d = din("ropeS", [128, 16, 48])
    tri_d = din("tri", [128, 2, 128])
    mask4_d = din("mask4", [128, 2, 4, 128], BF16)
    out_d = nc.dram_tensor("outT", [D, L], F32, kind="ExternalOutput").ap()
    tap_d = {}

    P = Prog()
    with contextlib.ExitStack() as es:
        def sb(name, shape, dt):
            return es.enter_context(nc.sbuf_tensor("sb_" + name, list(shape), dt))

        xT = sb("xT", [128, 8, NTOK], F32)
        hT = sb("hT", [128, 8, NTOK], BF16)
        mixg = sb("mixg", [128, 4, NTOK], BF16)
        WSL = 3072
        wsl = [sb("wsl%d" % i, [128, WSL], BF16) for i in range(2)]
        ARENA = 54 * 1024
        arena = sb("arena", [128, ARENA // 2], BF16)
        ident = sb("ident", [128, 128], BF16)
        ones_bf = sb("ones_bf", [128, 128], BF16)
        ones_f = sb("ones_f", [128, 2], F32)
        dft64 = sb("dft64", [128, 256], BF16)
        dftC = sb("dftC", [128, 2, 2, 256], BF16)
        ropeC = sb("ropeC", [128, 16, 48], F32)
        ropeS = sb("ropeS", [128, 16, 48], F32)
        tri = sb("tri", [128, 2, 128], F32)
        mask4 = sb("mask4", [128, 2, 4, 128], BF16)
        cvec = sb("cvec", [128, 8, 2], F32)
        sc_bf = sb("sc_bf", [128, 8, 2], BF16)
        mod_sb = sb("mod_sb", [128, 24, 2], F32)
        Gm = sb("Gm", [128, 8, 2], F32)
        b_ada = sb("b_ada", [128, 24], F32)
        norm_w = sb("norm_w", [128, 8], F32)
        normf = sb("normf", [128, 8], F32)
        gnw = sb("gnw", [96, 1], F32)
        walpha = sb("walpha", [33, 2, 192], F32)
        wfour = sb("wfour", [128, 2, 256], BF16)
        pbank = [es.enter_context(nc.psum_tensor("pb%d" % i, [128, 512], F32)) for i in range(8)]
        Bp = [Buf("pb%d" % i) for i in range(8)]

        def carve(off, nelem, dt, parts=128):
            nb = nelem * (4 if dt == F32 else 2)
            assert off % 4 == 0 and off + nb <= ARENA, (off, nb, ARENA)
            v = arena[0:parts, off // 2: (off + nb) // 2]
            return v.bitcast(F32) if dt == F32 else v

        Bx = [[Buf("x%d_%d" % (c, t)) for t in range(5)] for c in range(8)]
        Bh = [[Buf("h%d_%d" % (c, t)) for t in range(5)] for c in range(8)]
        Bm = [[Buf("m%d_%d" % (c, t)) for t in range(5)] for c in range(4)]
        Bw = [Buf("wsl0"), Buf("wsl1")]
        Bc = Buf("consts")
        Bmod = Buf("mod")
        state = {"ws": 0, "pb": 0}

        def next_slot():
            k = state["ws"]
            state["ws"] = 1 - k
            return wsl[k], Bw[k]

        def next_bank():
            k = state["pb"]
            state["pb"] = (k + 1) % 8
            return pbank[k], Bp[k]

        def mm(out, lhsT, rhs, start, stop, reads, writes):
            P.op("pe", lambda e: e.matmul(out, lhsT=lhsT, rhs=rhs, start=start, stop=stop), reads, writes)

        def act(out, in_, func, reads, writes, bias=None, scale=None):
            kw = {}
            if bias is not None:
                kw["bias"] = bias
            if scale is not None:
                kw["scale"] = scale
            P.op("act", lambda e: e.activation(out=out, in_=in_, func=func, **kw), reads, writes)

        def vtt(out, in0, in1, op, reads, writes, eng="dve"):
            P.op(eng, lambda e: e.tensor_tensor(out=out, in0=in0, in1=in1, op=op), reads, writes)

        def vts(out, in0, s1, op0, reads, writes, s2=None, op1=None, eng="dve"):
            if op1 is None:
                P.op(eng, lambda e: e.tensor_scalar(out=out, in0=in0, scalar1=s1, scalar2=None, op0=op0), reads, writes)
            else:
                P.op(eng, lambda e: e.tensor_scalar(out=out, in0=in0, scalar1=s1, scalar2=s2, op0=op0, op1=op1), reads, writes)

        def vstt(out, in0, scalar, in1, op0, op1, reads, writes):
            P.op("dve", lambda e: e.scalar_tensor_tensor(out=out, in0=in0, scalar=scalar, in1=in1, op0=op0, op1=op1), reads, writes)

        def vcopy(out, in_, reads, writes, eng="dve"):
            P.op(eng, lambda e: e.tensor_copy(out=out, in_=in_), reads, writes)

        def vrecip(out, in_, reads, writes):
            P.op("dve", lambda e: e.reciprocal(out=out, in_=in_), reads, writes)

        def memset(ap, val, writes, eng="pool"):
            P.op(eng, lambda e: e.memset(ap, val), (), writes)

        def tap(name, ap, shape, reads, dt=F32):
            if name in taps:
                t = nc.dram_tensor("tap_" + name, list(shape), dt, kind="ExternalOutput").ap()
                tap_d[name] = t
                P.dma("sp", t, ap, reads=reads)

        for c in range(8):
            P.dma("sp", xT[:, c, :], xT_d[c * 128:(c + 1) * 128, :], writes=[Bx[c][t] for t in range(5)])
        for dst, src in ((ident, ident_d), (dft64, dft64_d), (dftC, dftC_d), (ropeC, ropeC_d), (ropeS, ropeS_d),
                         (tri, tri_d), (mask4, mask4_d), (normf, normf_d)):
            P.dma("sp", dst[:], src, writes=[Bc])
        P.dma("sp", cvec[:, :, :].rearrange("p c v -> p (c v)"), cvec_d, writes=[Bc])
        memset(ones_bf[:], 1.0, [Bc])
        memset(ones_f[:], 1.0, [Bc])
        act(sc_bf[:, :, :], cvec[:, :, :], AF.Silu, [Bc], [Bc])

        def load_w(l, pieces):
            slot, bw = next_slot()
            off, ntot = WIN_LAY[tuple(pieces)]
            assert 8 * ntot <= WSL
            P.dma("pool", slot[:, 0:8 * ntot], w_in_d[l][:, off:off + 8 * ntot], writes=[bw], bar=False)
            return slot[:, 0:8 * ntot].rearrange("p (c n) -> p c n", c=8), bw

        def proj_fm(wv, bw, col0, ncols, ti, ps, bps, pparts=None):
            t0, tw, _ = TBS[ti]
            for c in range(8):
                mm(ps[0:ncols, 0:tw], wv[:, c, col0:col0 + ncols], hT[:, c, t0:t0 + tw], c == 0, c == 7,
                   [bw, Bh[c][ti]], [bps])

        def proj_tm(wv, bw, col0, ncols, j, ps, bps):
            ti = min(j // 4, 4)
            for c in range(8):
                mm(ps[:, 0:ncols], hT[:, c, j * 128:(j + 1) * 128], wv[:, c, col0:col0 + ncols], c == 0, c == 7,
                   [bw, Bh[c][ti]], [bps])

        def out_proj(l, row0, kparts, nk, srcs, tis):
            mh = 8 if nk * 1024 <= WSL else 4
            for m0 in range(0, 8, mh):
                slot, bw = next_slot()
                wv = slot[0:kparts, 0:nk * mh * 128].rearrange("p (k n) -> p k n", k=nk)
                woff = WOUT_BASE[row0] + (m0 // mh) * nk * mh * 128
                P.dma("pool", slot[0:kparts, 0:nk * mh * 128], w_out_d[l][0:kparts, woff:woff + nk * mh * 128], writes=[bw], bar=False)
                for ti in tis:
                    t0, tw, v = TBS[ti]
                    for m in range(m0, m0 + mh):
                        ps, bps = next_bank()
                        for k in range(nk):
                            mm(ps[:, 0:tw], wv[:, k, (m - m0) * 128:(m - m0 + 1) * 128], mixg[0:kparts, k, t0:t0 + tw],
                               k == 0, k == nk - 1, [bw, Bm[k][ti]], [bps])
                        vstt(xT[:, m, t0:t0 + tw], ps[:, 0:tw], mod_sb[:, 16 + m, v:v + 1], xT[:, m, t0:t0 + tw],
                             ALU.mult, ALU.add, [bps, Bmod, Bx[m][ti]], [Bx[m][ti]])

        for l in range(n_layers):
            last = (l == 3)
            tis_all = [0, 1, 2, 3, 4]
            tis_out = [0, 1, 2, 3] if last else tis_all

            P.phase = 'L%d.ada' % l
            P.dma("sp", b_ada[:], b_ada_d[l], writes=[Bmod])
            P.dma("sp", norm_w[:], norm_w_d[l], writes=[Bmod])
            P.dma("sp", gnw[:], gnw_d[l], writes=[Bmod])
            P.dma("sp", walpha[:, :, :].rearrange("p a b -> p (a b)"), walpha_d[l], writes=[Bmod])
            P.dma("pool", wfour[:, :, :], w_four_d[l].rearrange("(k p) n -> p k n", p=128), writes=[Bmod])
            mps, bmps = next_bank()
            mview = mps[:, 0:48].rearrange("p (j v) -> p j v", v=2)
            for pj in range(8):
                slot, bw = next_slot()
                wv = slot[:, 0:8 * 384].rearrange("p (c n) -> p c n", c=8)
                P.dma("pool", slot[:, 0:8 * 384], w_ada_d[l][:, pj * 3072:(pj + 1) * 3072], writes=[bw], bar=False)
                for j3 in range(3):
                    j = pj * 3 + j3
                    for c in range(8):
                        mm(mview[:, j, :], wv[:, c, j3 * 128:(j3 + 1) * 128], sc_bf[:, c, :], c == 0, c == 7,
                           [bw, Bc], [bmps])
            for v in range(2):
                vtt(mod_sb[:, :, v], mview[:, :, v], b_ada[:, :], ALU.add, [bmps, Bmod], [Bmod])
            for v in range(2):
                vstt(Gm[:, :, v], mod_sb[:, 8:16, v], 1.0, norm_w[:, :], ALU.add, ALU.mult, [Bmod], [Bmod])
            tap("mod%d" % l, mod_sb[:, :, :].rearrange("p j v -> p (j v)"), [128, 48], [Bmod])

            if l > 0:
                P.barrier()
            P.phase = 'L%d.norm' % l
            sqs = [carve(8192 * i, 8 * 512, BF16).rearrange("p (c n) -> p c n", c=8) for i in range(2)]
            rstds = [carve(16384 + 2048 * i, 512, F32) for i in range(2)]
            tmpn = [carve(20480 + 2048 * i, 512, F32) for i in range(4)]
            Bsqs = [[Buf() for _ in range(8)] for _ in range(2)]
            Brss, Btn = [Buf(), Buf()], [Buf() for _ in range(4)]

            def n_stat(ti):
                t0, tw, v = TBS[ti]
                sq, bsq = sqs[ti % 2], Bsqs[ti % 2]
                for c in range(8):
                    if c % 4 != 3:
                        act(sq[:, c, 0:tw], xT[:, c, t0:t0 + tw], AF.Square, [Bx[c][ti]], [bsq[c]])
                    else:
                        vtt(sq[:, c, 0:tw], xT[:, c, t0:t0 + tw], xT[:, c, t0:t0 + tw], ALU.mult, [Bx[c][ti]], [bsq[c]])
                ps, bps = next_bank()
                for c in range(8):
                    mm(ps[:, 0:tw], ones_bf[:, :], sq[:, c, 0:tw], c == 0, c == 7, [bsq[c], Bc], [bps])
                act(rstds[ti % 2][:, 0:tw], ps[:, 0:tw], AF.Sqrt, [bps], [Brss[ti % 2]], bias=EPS, scale=1.0 / D)
                vrecip(rstds[ti % 2][:, 0:tw], rstds[ti % 2][:, 0:tw], [Brss[ti % 2]], [Brss[ti % 2]])

            def n_mod(ti):
                t0, tw, v = TBS[ti]
                rstd, brs = rstds[ti % 2], Brss[ti % 2]
                for c in range(8):
                    k = c % 4
                    vtt(tmpn[k][:, 0:tw], xT[:, c, t0:t0 + tw], rstd[:, 0:tw], ALU.mult, [Bx[c][ti], brs], [Btn[k]],
                        eng=("dve" if c % 2 == 0 else "pool"))
                    act(hT[:, c, t0:t0 + tw], tmpn[k][:, 0:tw], AF.Identity, [Btn[k], Bmod], [Bh[c][ti]],
                        bias=mod_sb[:, c, v:v + 1], scale=Gm[:, c, v:v + 1])
            n_stat(0)
            for ti in range(5):
                if ti + 1 < 5:
                    n_stat(ti + 1)
                n_mod(ti)
            if l == 0:
                tap("hT", hT[:, :, :].rearrange("p c n -> p (c n)"), [128, 8 * NTOK], [Bh[c][t] for c in range(8) for t in range(5)], BF16)
            preA = load_w(l, ((256, 256),))
            P.barrier()

            if "A" not in skip:
                P.phase = 'L%d.A' % l
                tis = tis_out
                wv, bw = preA
                for m in range(2):
                    for ti in tis:
                        t0, tw, _ = TBS[ti]
                        ps, bps = next_bank()
                        proj_fm(wv, bw, m * 128, 128, ti, ps, bps)
                        act(mixg[:, m, t0:t0 + tw], ps[:, 0:tw], AF.Silu, [bps], [Bm[m][ti]])
                P.phase = 'L%d.A.u' % l
                uT = carve(0, 2 * NTOK, BF16).rearrange("p (m n) -> p m n", m=2)
                BuT = [[Buf() for _ in range(5)] for _ in range(2)]
                wv, bw = load_w(l, ((0, 256),))
                for m in range(2):
                    for ti in tis:
                        t0, tw, _ = TBS[ti]
                        ps, bps = next_bank()
                        proj_fm(wv, bw, m * 128, 128, ti, ps, bps)
                        vcopy(uT[:, m, t0:t0 + tw], ps[:, 0:tw], [bps], [BuT[m][ti]])
                P.phase = 'L%d.A.ucs' % l
                uCS = carve(9216, NT * 512, BF16).rearrange("p (j n) -> p j n", j=NT)
                BuCS = [Buf() for _ in range(NT)]
                njt = NT if not last else 16
                for j in range(njt):
                    ti = min(j // 4, 4)
                    ps, bps = next_bank()
                    for k in range(2):
                        mm(ps[:, k * 128:(k + 1) * 128], uT[:, k, j * 128:(j + 1) * 128], dft64[:, 0:128], True, True,
                           [BuT[k][ti], Bc], [bps])
                        mm(ps[:, 256 + k * 128:256 + (k + 1) * 128], uT[:, k, j * 128:(j + 1) * 128], dft64[:, 128:256], True, True,
                           [BuT[k][ti], Bc], [bps])
                    if j % 2 == 0:
                        act(uCS[:, j, :], ps[:, :], AF.Copy, [bps], [BuCS[j]])
                    else:
                        vcopy(uCS[:, j, :], ps[:, :], [bps], [BuCS[j]])
                P.phase = 'L%d.A.dft' % l
                dbuf = [carve(27648 + 8192 * i, 2 * L, BF16).rearrange("p (a n) -> p a n", a=2) for i in range(2)]
                Bdb = [Buf(), Buf()]
                for k in range(16):
                    db, bdb = dbuf[k % 2], Bdb[k % 2]
                    P.dma("sp", db[:, 0, :], dftL_d[0, k * 128:(k + 1) * 128, :], writes=[bdb])
                    P.dma("sp", db[:, 1, :], dftL_d[1, k * 128:(k + 1) * 128, :], writes=[bdb])
                    for m in range(2):
                        for n in range(4):
                            b = m * 4 + n
                            mm(pbank[b][:, :], uCS[:, k, m * 128:(m + 1) * 128], db[:, 0, n * 512:(n + 1) * 512], k == 0, False,
                               [BuCS[k], bdb], [Bp[b]])
                            mm(pbank[b][:, :], uCS[:, k, 256 + m * 128:256 + (m + 1) * 128], db[:, 1, n * 512:(n + 1) * 512], False, k == 15,
                               [BuCS[k], bdb], [Bp[b]])
                fT = uT
                BfT = BuT
                for m in range(2):
                    for n in range(4):
                        b = m * 4 + n
                        if b % 2 == 0:
                            act(fT[:, m, n * 512:(n + 1) * 512], pbank[b][:, :], AF.Copy, [Bp[b]] + BuCS, [BfT[m][n]])
                        else:
                            vcopy(fT[:, m, n * 512:(n + 1) * 512], pbank[b][:, :], [Bp[b]] + BuCS, [BfT[m][n]])
                if not last:
                    for m in range(2):
                        ps, bps = next_bank()
                        for k in range(2):
                            mm(ps[:, 0:256], uCS[:, 16 + k, m * 128:(m + 1) * 128], dftC[:, k, 0, :], k == 0, False,
                               [BuCS[16 + k], Bc], [bps])
                            mm(ps[:, 0:256], uCS[:, 16 + k, 256 + m * 128:256 + (m + 1) * 128], dftC[:, k, 1, :], False, k == 1,
                               [BuCS[16 + k], Bc], [bps])
                        vcopy(fT[:, m, L:NTOK], ps[:, 0:256], [bps] + BuCS, [BfT[m][4]])
                P.phase = 'L%d.A.four' % l
                for m2 in range(2):
                    for ti in tis:
                        t0, tw, _ = TBS[ti]
                        ps, bps = next_bank()
                        for m in range(2):
                            mm(ps[:, 0:tw], wfour[:, m, m2 * 128:(m2 + 1) * 128], fT[:, m, t0:t0 + tw], m == 0, m == 1,
                               [Bmod, BfT[m][ti]], [bps])
                        vtt(mixg[:, m2, t0:t0 + tw], ps[:, 0:tw], mixg[:, m2, t0:t0 + tw], ALU.mult, [bps, Bm[m2][ti]], [Bm[m2][ti]])
                if l == 0:
                    tap("mixA", mixg[:, 0:2, :].rearrange("p c n -> p (c n)"), [128, 2 * NTOK], [Bm[c][t] for c in range(2) for t in range(5)], BF16)
                P.phase = 'L%d.A.out' % l
                out_proj(l, 0, 128, 2, None, tis)
                preB = load_w(l, ((1664, 384),))
                P.barrier()

            if "B" not in skip:
                P.phase = 'L%d.B' % l
                tis = tis_out
                wv, bw = preB
                for m in range(3):
                    for ti in tis:
                        t0, tw, _ = TBS[ti]
                        ps, bps = next_bank()
                        proj_fm(wv, bw, m * 128, 128, ti, ps, bps)
                        act(mixg[:, m, t0:t0 + tw], ps[:, 0:tw], AF.Silu, [bps], [Bm[m][ti]])
                qT = carve(0, NTOK, BF16)
                kT = carve(4608, NTOK, BF16)
                vtk = carve(9216, NT * 130, BF16).rearrange("p (j h d) -> p j h d", j=NT, h=2)
                EB = carve(13896, 2 * 21 * 128, BF16).rearrange("p (h t q) -> p h t q", h=2, t=21)
                stage = carve(24648, 21 * 128, F32).rearrange("p (t q) -> p t q", t=21)
                NE = 6
                Et = [carve(35400 + 1792 * i, 7 * 128, BF16).rearrange("p (t q) -> p t q", t=7) for i in range(NE)]
                osb = [carve(46152 + 256 * i, 128, BF16) for i in range(2)]
                rden = [carve(46664 + 8 * i, 2, F32) for i in range(2)]
                Bq = [Buf() for _ in range(5)]
                Bk = [Buf() for _ in range(5)]
                Bv = [Buf() for _ in range(NT)]
                BEB, Bst = [Buf(), Buf()], Buf()
                BE = [Buf() for _ in range(NE)]
                Bos, Brd = [Buf(), Buf()], [Buf(), Buf()]
                for j in range(NT):
                    memset(vtk[:, j, :, 64:65], 1.0, [Bv[j]])
                P.phase = 'L%d.B.proj' % l
                for i in range(3):
                    wv, bw = load_w(l, ((512 + 128 * i, 128), (896 + 128 * i, 128), (1280 + 128 * i, 128),))
                    for ti in tis_all:
                        t0, tw, _ = TBS[ti]
                        if ti in tis:
                            ps, bps = next_bank()
                            proj_fm(wv, bw, 0, 128, ti, ps, bps)
                            act(qT[:, t0:t0 + tw], ps[:, 0:tw], AF.Copy, [bps], [Bq[ti]])
                        ps, bps = next_bank()
                        proj_fm(wv, bw, 128, 128, ti, ps, bps)
                        vcopy(kT[:, t0:t0 + tw], ps[:, 0:tw], [bps], [Bk[ti]])
                    for j in range(NT):
                        ps, bps = next_bank()
                        proj_tm(wv, bw, 256, 128, j, ps, bps)
                        src = ps[:, 0:128].rearrange("p (h d) -> p h d", h=2)
                        if j % 2 == 0:
                            act(vtk[:, j, :, 0:64], src, AF.Copy, [bps], [Bv[j]])
                        else:
                            vcopy(vtk[:, j, :, 0:64], src, [bps], [Bv[j]])
                    P.phase = 'L%d.B.proj' % l
                    for h2 in range(2):
                        P.dma("sp", stage[:, :, :], biasT_d[l, 2 * i + h2], writes=[Bst])
                        act(EB[:, h2, :, :], stage[:, :, :], AF.Exp, [Bst], [BEB[h2]])
                    P.phase = 'L%d.B.att' % l
                    items = []
                    for j in range(16):
                        kts, e0 = _na_keytiles(j)
                        items.append((j, kts, e0, True))
                    if not last:
                        for j in (16, 17):
                            items.append((j, [], 0, False))
                    work = [(it, h2) for it in items for h2 in range(2)]

                    def stage1(n):
                        (j, kts, e0, loc), h2 = work[n]
                        base = 64 * h2
                        E, bE = Et[n % NE], BE[n % NE]
                        allk = kts + [16, 17]
                        nl = len(kts)
                        tq = min(j // 4, 4)
                        psA, bA = next_bank()
                        psB, bB = (None, None)
                        if len(allk) > 4:
                            psB, bB = next_bank()
                        for t, kt in enumerate(allk):
                            ps, bps = (psA, bA) if t < 4 else (psB, bB)
                            tt = t % 4
                            mm(ps[:, tt * 128:(tt + 1) * 128], kT[base:base + 64, kt * 128:(kt + 1) * 128],
                               qT[base:base + 64, j * 128:(j + 1) * 128], True, True, [Bk[min(kt // 4, 4)], Bq[tq]], [bps])
                        na = min(4, len(allk))
                        act(E[:, 0:na, :], psA[:, 0:na * 128].rearrange("p (t q) -> p t q", t=na), AF.Exp, [bA], [bE], scale=0.125)
                        if len(allk) > 4:
                            nb = len(allk) - 4
                            act(E[:, 4:4 + nb, :], psB[:, 0:nb * 128].rearrange("p (t q) -> p t q", t=nb), AF.Exp, [bB], [bE], scale=0.125)
                        if nl:
                            vtt(E[:, 0:nl, :], E[:, 0:nl, :], EB[:, h2, e0:e0 + nl, :], ALU.mult, [bE, BEB[h2]], [bE])
                        return allk

                    def stage2(n, allk, ops_, bops):
                        (j, kts, e0, loc), h2 = work[n]
                        E, bE = Et[n % NE], BE[n % NE]
                        for t, kt in enumerate(allk):
                            mm(ops_[:, h2 * 128:h2 * 128 + 65], E[:, t, :], vtk[:, kt, h2, :], t == 0, t == len(allk) - 1,
                               [bE, Bv[kt]], [bops])
                        if h2 == 1:
                            pr = (n // 2) % 2
                            ov = ops_[:, 0:256].rearrange("p (h d) -> p h d", h=2)
                            vrecip(rden[pr][:, :], ov[:, :, 64], [bops], [Brd[pr]])
                            for hh in range(2):
                                vts(osb[pr][:, hh * 64:(hh + 1) * 64], ov[:, hh, 0:64], rden[pr][:, hh:hh + 1], ALU.mult,
                                    [bops, Brd[pr]], [Bos[pr]])
                            pt, bpt = next_bank()
                            ptb = pt[:, 0:64].bitcast(BF16)
                            P.op("pe", lambda e: e.transpose(out=ptb, in_=osb[pr][:, :], identity=ident[:, :]), [Bos[pr], Bc], [bpt])
                            tq = min(j // 4, 4)
                            vtt(mixg[:, i, j * 128:(j + 1) * 128], ptb, mixg[:, i, j * 128:(j + 1) * 128], ALU.mult,
                                [bpt, Bm[i][tq]], [Bm[i][tq]])

                    DEPTH = 2
                    allks = {}
                    cur_o = None
                    for n in range(len(work) + DEPTH):
                        if n < len(work):
                            allks[n] = stage1(n)
                        pn = n - DEPTH
                        if pn >= 0:
                            if work[pn][1] == 0:
                                cur_o = next_bank()
                            stage2(pn, allks.pop(pn), cur_o[0], cur_o[1])
                if l == 0:
                    tap("mixB", mixg[:, 0:3, :].rearrange("p c n -> p (c n)"), [128, 3 * NTOK], [Bm[c][t] for c in range(3) for t in range(5)], BF16)
                P.phase = 'L%d.B.out' % l
                out_proj(l, 256, 128, 3, None, tis)
                preC = load_w(l, ((2816, 384),))
                P.barrier()

            if "C" not in skip:
                P.phase = 'L%d.C' % l
                tis = tis_out
                wv, bw = preC
                for h in range(4):
                    for ti in tis:
                        t0, tw, _ = TBS[ti]
                        ps, bps = next_bank()
                        proj_fm(wv, bw, h * 96, 96, ti, ps, bps)
                        act(mixg[0:96, h, t0:t0 + tw], ps[0:96, 0:tw], AF.Silu, [bps], [Bm[h][ti]])
                qkr = carve(0, NT * 384, BF16).rearrange("p (j g d) -> p j g d", j=NT, g=8)
                vg = carve(13824, NT * 384, BF16).rearrange("p (j n) -> p j n", j=NT)
                zT = carve(27648, NTOK, F32, parts=33)
                off = 36864
                Bqk = [Buf() for _ in range(NT)]
                Bvg = [Buf() for _ in range(NT)]
                Bz = [Buf() for _ in range(5)]

                def cv(n, dt, parts=128):
                    nonlocal off
                    v = carve(off, n, dt, parts)
                    off += n * (4 if dt == F32 else 2)
                    off = (off + 3) // 4 * 4
                    return v
                t1 = cv(384, F32)
                t2 = cv(384, F32)
                Bt1, Bt2 = Buf(), Buf()
                P.phase = 'L%d.C.proj' % l
                wv, bw = load_w(l, ((2048, 384),))
                for j in range(NT):
                    ps, bps = next_bank()
                    proj_tm(wv, bw, 0, 384, j, ps, bps)
                    if j < 16:
                        cb = ropeC[:, j, :].unsqueeze(1).to_broadcast([128, 8, 48])
                        vtt(t1[:, :].rearrange("p (g d) -> p g d", g=8), ps[:, 0:384].rearrange("p (g d) -> p g d", g=8), cb,
                            ALU.mult, [bps, Bc], [Bt1])
                        psv = ps[:, 0:384].rearrange("p (g f u w) -> p g f u w", g=8, f=2, u=2)
                        t2v = t2[:, :].rearrange("p (g f u w) -> p g f u w", g=8, f=2, u=2)
                        sv = ropeS[:, j, :].rearrange("p (f u w) -> p f u w", f=2, u=2)
                        for u in range(2):
                            vtt(t2v[:, :, :, u, :], psv[:, :, :, 1 - u, :], sv[:, :, u, :].unsqueeze(1).to_broadcast([128, 8, 2, 12]),
                                ALU.mult, [bps, Bc], [Bt2])
                        vtt(qkr[:, j, :, :], t1[:, :].rearrange("p (g d) -> p g d", g=8), t2[:, :].rearrange("p (g d) -> p g d", g=8),
                            ALU.add, [Bt1, Bt2], [Bqk[j]])
                    else:
                        vcopy(qkr[:, j, :, :], ps[:, 0:384].rearrange("p (g d) -> p g d", g=8), [bps], [Bqk[j]])
                wv, bw = load_w(l, ((2432, 384),))
                for j in range(NT):
                    ps, bps = next_bank()
                    proj_tm(wv, bw, 0, 384, j, ps, bps)
                    if j % 2 == 0:
                        act(vg[:, j, :], ps[:, 0:384], AF.Copy, [bps], [Bvg[j]])
                    else:
                        vcopy(vg[:, j, :], ps[:, 0:384], [bps], [Bvg[j]])
                wv, bw = load_w(l, ((3200, 32),))
                memset(zT[32:33, :], 1.0, Bz)
                for ti in tis_all:
                    t0, tw, _ = TBS[ti]
                    ps, bps = next_bank()
                    proj_fm(wv, bw, 0, 32, ti, ps, bps)
                    vcopy(zT[0:32, t0:t0 + tw], ps[0:32, 0:tw], [bps], [Bz[ti]])
                P.barrier()
                oacc = hT[0:96, :, :].rearrange("p c n -> p (c n)").bitcast(F32).rearrange("p (h n) -> p h n", h=4)
                Boa = [Buf() for _ in range(NT)]
                class _Al:
                    def __init__(self, ap2d, nbytes, start=0):
                        self.ap, self.n, self.off = ap2d, nbytes, start

                    def __call__(self, nelem, dt, parts=128):
                        nb = nelem * (4 if dt == F32 else 2)
                        assert self.off % 4 == 0 and self.off + nb <= self.n, (self.off, nb, self.n)
                        v = self.ap[0:parts, self.off // 2:(self.off + nb) // 2]
                        self.off = (self.off + nb + 3) // 4 * 4
                        return v.bitcast(F32) if dt == F32 else v
                tot = carve(36864, 512, F32, 96)
                sqo = carve(36864 + 2048, 512, BF16, 96)
                al0 = _Al(arena, ARENA, 39936)
                alA, alB = _Al(wsl[0], 2 * WSL), _Al(wsl[1], 2 * WSL)
                Btot, Bsqo = Buf(), Buf()
                LNS = float(np.log(48.0 ** -0.5))
                SC = []
                for d in range(2):
                    a1, a2 = (al0, al0) if d == 0 else (alA, alB)
                    sc = {}
                    sc["spd"] = [a1(256, F32) for _ in range(2)]
                    sc["eq"] = [a1(192, F32)]
                    sc["ek"] = [a1(192, F32)]
                    sc["qd"] = [a1(256, BF16) for _ in range(2)]
                    sc["kd"] = [a1(256, BF16) for _ in range(3)]
                    sc["qkT"] = [a2(512, BF16) for _ in range(2)]
                    sc["Am"] = [a2(512, BF16) for _ in range(2)]
                    sc["ebend"] = [a2(2, F32) for _ in range(3)]
                    sc["st_f"] = a2(192, F32)
                    sc["st_t"] = a2(192, F32)
                    sc["st_b"] = a2(192, BF16)
                    sc["B"] = {k: [Buf() for _ in v] for k, v in sc.items() if isinstance(v, list)}
                    sc["Bst_f"], sc["Bst_t"], sc["Bst_b"] = Buf(), Buf(), Buf()
                    for k in ("spd", "qd", "kd"):
                        for i_, t_ in enumerate(sc[k]):
                            memset(t_[:, :], 0.0, [sc["B"][k][i_]])
                    memset(sc["st_f"][:, :], 0.0, [sc["Bst_f"]])
                    memset(sc["st_b"][:, :], 0.0, [sc["Bst_b"]])
                    sc["order"] = [16, 17] + list(range(16)) if d == 0 else [17, 16] + list(range(15, -1, -1))
                    SC.append(sc)
                P.phase = 'L%d.C.sweep' % l
                stored = set()

                def h4(ap):
                    return ap.rearrange("p (h d) -> p h d", h=4)

                def st_a(d, n):
                    sc = SC[d]; c = sc["order"][n]; ti = min(c // 4, 4)
                    spd, bspd = sc["spd"][n % 2], sc["B"]["spd"][n % 2]
                    gps, bg = next_bank()
                    mm(gps[:, 0:192], zT[0:33, c * 128:(c + 1) * 128], walpha[0:33, d, :], True, True, [Bz[ti], Bmod], [bg])
                    spv = h4(spd[:, :])[:, :, 0:48]
                    act(spv, h4(gps[:, 0:192]), AF.Exp, [bg], [bspd], scale=-1.0)
                    act(spv, spv, AF.Ln, [bspd], [bspd], bias=1.0)

                def st_b(d, n):
                    sc = SC[d]; c = sc["order"][n]
                    spd, bspd = sc["spd"][n % 2], sc["B"]["spd"][n % 2]
                    eq, beq = sc["eq"][0], sc["B"]["eq"][0]
                    ek, bek = sc["ek"][0], sc["B"]["ek"][0]
                    qd, bqd = sc["qd"][n % 2], sc["B"]["qd"][n % 2]
                    kd, bkd = sc["kd"][n % 3], sc["B"]["kd"][n % 3]
                    eb, beb = sc["ebend"][n % 3], sc["B"]["ebend"][n % 3]
                    bps_, bb = next_bank()
                    mm(bps_[:, 0:256], tri[:, d, :], spd[:, :], True, True, [Bc, bspd], [bb])
                    for i in range(2):
                        mm(bps_[:, 256 + 2 * i:258 + 2 * i], spd[:, i * 128:(i + 1) * 128], ones_f[:, 0:2], True, True, [bspd, Bc], [bb])
                    bv = h4(bps_[:, 0:256])[:, :, 0:48]
                    act(h4(eq[:, :]), bv, AF.Exp, [bb], [beq], bias=LNS, scale=-1.0 / 16)
                    act(h4(ek[:, :]), bv, AF.Exp, [bb], [bek], scale=1.0 / 16)
                    act(eb[:, :], bps_[:, 256:260].rearrange("p (i two) -> p i two", two=2)[:, :, 0], AF.Exp, [bb], [beb], scale=-1.0 / 16)
                    vtt(h4(qd[:, :])[:, :, 0:48], qkr[:, c, 0:4, :], h4(eq[:, :]), ALU.mult, [Bqk[c], beq], [bqd])
                    vtt(h4(kd[:, :])[:, :, 0:48], qkr[:, c, 4:8, :], h4(ek[:, :]), ALU.mult, [Bqk[c], bek], [bkd])

                def st_c(d, n):
                    sc = SC[d]
                    qd, bqd = sc["qd"][n % 2], sc["B"]["qd"][n % 2]
                    kd, bkd = sc["kd"][n % 3], sc["B"]["kd"][n % 3]
                    qkT, bqkT = sc["qkT"][n % 2], sc["B"]["qkT"][n % 2]
                    tp, btp = next_bank()
                    tpb = tp[:, 0:256].bitcast(BF16)
                    for i in range(2):
                        P.op("pe", lambda e, a=tpb[:, i * 128:(i + 1) * 128], b=qd[:, i * 128:(i + 1) * 128]:
                             e.transpose(out=a, in_=b, identity=ident[:, :]), [bqd, Bc], [btp])
                        P.op("pe", lambda e, a=tpb[:, (2 + i) * 128:(3 + i) * 128], b=kd[:, i * 128:(i + 1) * 128]:
                             e.transpose(out=a, in_=b, identity=ident[:, :]), [bkd, Bc], [btp])
                    act(qkT[:, :], tpb, AF.Copy, [btp], [bqkT])

                def st_d(d, n):
                    sc = SC[d]
                    qkT, bqkT = sc["qkT"][n % 2], sc["B"]["qkT"][n % 2]
                    Am, bAm = sc["Am"][n % 2], sc["B"]["Am"][n % 2]
                    apsb = [next_bank(), next_bank()]
                    Amv = Am[:, :].rearrange("p (i b t) -> p i b t", i=2, b=2)
                    for b in range(2):
                        aps, ba = apsb[b]
                        base = 64 * b
                        for i in range(2):
                            mm(aps[:, i * 128:(i + 1) * 128], qkT[base:base + 64, (2 + i) * 128:(3 + i) * 128],
                               qkT[base:base + 64, i * 128:(i + 1) * 128], True, True, [bqkT], [ba])
                    for b in range(2):
                        aps, ba = apsb[b]
                        vtt(Amv[:, :, b, :], aps[:, 0:256].rearrange("p (i t) -> p i t", i=2), mask4[:, d, 0:2, :], ALU.mult,
                            [ba, Bc], [bAm])

                def st_e(d, n):
                    sc = SC[d]; c = sc["order"][n]; ti = min(c // 4, 4)
                    kd, bkd = sc["kd"][n % 3], sc["B"]["kd"][n % 3]
                    qkT, bqkT = sc["qkT"][n % 2], sc["B"]["qkT"][n % 2]
                    Am, bAm = sc["Am"][n % 2], sc["B"]["Am"][n % 2]
                    eb, beb = sc["ebend"][n % 3], sc["B"]["ebend"][n % 3]
                    st_f, st_t, stb = sc["st_f"], sc["st_t"], sc["st_b"]
                    opsb = [next_bank(), next_bank()]
                    for b in range(2):
                        ops_, bo = opsb[b]
                        base = 64 * b
                        for i in range(2):
                            h = 2 * i + b
                            mm(ops_[0:96, i * 128:(i + 1) * 128], vg[:, c, h * 96:(h + 1) * 96], Am[:, h * 128:(h + 1) * 128], True, False,
                               [Bvg[c], bAm], [bo])
                            mm(ops_[0:96, i * 128:(i + 1) * 128], stb[base:base + 64, i * 96:(i + 1) * 96],
                               qkT[base:base + 64, i * 128:(i + 1) * 128], False, True, [sc["Bst_b"], bqkT], [bo])
                    ups, bu = next_bank()
                    for h in range(4):
                        i, base = h // 2, 64 * (h % 2)
                        mm(ups[base:base + 64, i * 96:(i + 1) * 96], kd[:, h * 64:(h + 1) * 64], vg[:, c, h * 96:(h + 1) * 96],
                           True, True, [bkd, Bvg[c]], [bu])
                    vtt(st_t[:, :], ups[:, 0:192], st_f[:, :], ALU.add, [bu, sc["Bst_f"]], [sc["Bst_t"]])
                    for i in range(2):
                        act(stb[:, i * 96:(i + 1) * 96], st_t[:, i * 96:(i + 1) * 96], AF.Identity, [sc["Bst_t"], beb], [sc["Bst_b"]],
                            scale=eb[:, i:i + 1])
                    for i in range(2):
                        vts(st_f[:, i * 96:(i + 1) * 96], st_t[:, i * 96:(i + 1) * 96], eb[:, i:i + 1], ALU.mult,
                            [sc["Bst_t"], beb], [sc["Bst_f"]])
                    oav = oacc[:, :, c * 128:(c + 1) * 128].rearrange("p (i b) t -> p i b t", b=2)
                    totv = tot[:, :].rearrange("p (i b t) -> p i b t", i=2, b=2)
                    if c not in stored:
                        stored.add(c)
                        for b in range(2):
                            ops_, bo = opsb[b]
                            ov = ops_[0:96, 0:256].rearrange("p (i t) -> p i t", i=2)
                            if b == 0:
                                act(oav[:, :, b, :], ov, AF.Copy, [bo], [Boa[c]])
                            else:
                                vcopy(oav[:, :, b, :], ov, [bo], [Boa[c]])
                    elif not (last and c >= 16):
                        for b in range(2):
                            ops_, bo = opsb[b]
                            ov = ops_[0:96, 0:256].rearrange("p (i t) -> p i t", i=2)
                            vtt(totv[:, :, b, :], ov, oav[:, :, b, :], ALU.add, [bo, Boa[c]], [Btot])
                        act(sqo[:, :], tot[:, :], AF.Square, [Btot], [Bsqo])
                        sps, bs_ = next_bank()
                        mm(sps[0:96, :], ones_bf[0:96, 0:96], sqo[:, :], True, True, [Bsqo, Bc], [bs_])
                        act(sps[0:96, :], sps[0:96, :], AF.Sqrt, [bs_], [bs_], bias=EPS, scale=1.0 / 96)
                        vrecip(sps[0:96, :], sps[0:96, :], [bs_], [bs_])
                        vtt(tot[:, :], tot[:, :], sps[0:96, :], ALU.mult, [Btot, bs_], [Btot])
                        mv = mixg[0:96, :, c * 128:(c + 1) * 128]
                        vstt(mv, tot[:, :].rearrange("p (h t) -> p h t", h=4), gnw[:, 0:1], mv, ALU.mult, ALU.mult,
                             [Btot, Bmod] + [Bm[h][ti] for h in range(4)], [Bm[h][ti] for h in range(4)])

                def ok(n):
                    return 0 <= n < NT
                for t in range(-3, NT):
                    for fn, dn in ((st_a, 3), (st_b, 2), (st_c, 1), (st_e, 0), (st_d, 1)):
                        for d in range(2):
                            if ok(t + dn) and ('C' + fn.__name__[-1]) not in skip:
                                fn(d, t + dn)
                P.barrier()
                if l == 0:
                    tap("mixC", mixg[0:96, :, :].rearrange("p c n -> p (c n)"), [96, 4 * NTOK], [Bm[c][t] for c in range(4) for t in range(5)], BF16)
                P.phase = 'L%d.C.out' % l
                out_proj(l, 640, 96, 4, None, tis)
                if l == n_layers - 1:
                    P.barrier()
            if l == 0:
                tap("x1", xT[:, :, :].rearrange("p c n -> p (c n)"), [128, 8 * NTOK], [Bx[c][t] for c in range(8) for t in range(5)])

        P.phase = 'final'
        sq = carve(0, 8 * 512, BF16).rearrange("p (c n) -> p c n", c=8)
        rstd = carve(8192, 512, F32)
        ost = [carve(10240 + 2048 * i, 512, F32) for i in range(4)]
        Bsq, Brs, Bos_ = Buf("sq"), Buf("rstd"), [Buf() for _ in range(4)]
        for ti in range(4):
            t0, tw, v = TBS[ti]
            for c in range(8):
                act(sq[:, c, 0:tw], xT[:, c, t0:t0 + tw], AF.Square, [Bx[c][ti]], [Bsq])
            ps, bps = next_bank()
            for c in range(8):
                mm(ps[:, 0:tw], ones_bf[:, :], sq[:, c, 0:tw], c == 0, c == 7, [Bsq, Bc], [bps])
            act(rstd[:, 0:tw], ps[:, 0:tw], AF.Sqrt, [bps], [Brs], bias=EPS, scale=1.0 / D)
            vrecip(rstd[:, 0:tw], rstd[:, 0:tw], [Brs], [Brs])
            for c in range(8):
                k = c % 4
                vstt(ost[k][:, 0:tw], xT[:, c, t0:t0 + tw], normf[:, c:c + 1], rstd[:, 0:tw], ALU.mult, ALU.mult,
                     [Bx[c][ti], Brs, Bc], [Bos_[k]])
                P.dma("sp", out_d[c * 128:(c + 1) * 128, t0:t0 + tw], ost[k][:, 0:tw], reads=[Bos_[k]])
        stats = P.emit(nc)
    _NC_CACHE['prog'] = P
    return nc, stats, tap_d


_NC_CACHE = {}


def prep_inputs(x, c, ctx, c_ctx, w_ada, b_ada, norm_w, w_in, w_four, rpb, w_alpha_fwd, b_alpha_fwd,
                w_alpha_bwd, b_alpha_bwd, gla_norm_w, w_out, norm_f, cores):
    C = _consts()
    f32 = np.float32
    x = np.asarray(x, f32); ctx = np.asarray(ctx, f32); c = np.asarray(c, f32); c_ctx = np.asarray(c_ctx, f32)
    rpb = np.asarray(rpb, f32)
    nl = rpb.shape[0]
    rpb_pad = np.concatenate([rpb.reshape(nl, 6, -1), np.full((nl, 6, 1), PAD_BIAS, f32)], axis=-1)
    biasT = rpb_pad[:, :, C["naidx"]]
    biasT = np.ascontiguousarray(biasT.transpose(0, 1, 3, 2, 4))
    walpha = np.zeros((nl, 33, 2, 192), f32)
    walpha[:, 0:16, 0, :] = np.asarray(w_alpha_fwd, f32)
    walpha[:, 16:32, 1, :] = np.asarray(w_alpha_bwd, f32)
    walpha[:, 32, 0, :] = np.asarray(b_alpha_fwd, f32)
    walpha[:, 32, 1, :] = np.asarray(b_alpha_bwd, f32)
    w_in = np.asarray(w_in, f32); w_ada = np.asarray(w_ada, f32); w_out = np.asarray(w_out, f32)
    pieces, _lay = _win_pieces()
    w_in_r = np.empty((nl, 128, 8 * DIN), f32)
    w_ada_r = np.empty((nl, 128, 8 * 3 * D), f32)
    w_out_r = np.zeros((nl, 128, 9216), f32)
    for l_ in range(nl):
        Wl = w_in[l_].reshape(8, 128, DIN).transpose(1, 0, 2)
        o_ = 0
        for p_ in pieces:
            blk = np.concatenate([Wl[:, :, c0:c0 + k] for c0, k in p_], axis=2)
            w_in_r[l_, :, o_:o_ + blk.shape[1] * blk.shape[2]] = blk.reshape(128, -1)
            o_ += blk.shape[1] * blk.shape[2]
        Wa = w_ada[l_].reshape(8, 128, 3 * D).transpose(1, 0, 2)
        for pj in range(8):
            w_ada_r[l_, :, pj * 3072:(pj + 1) * 3072] = Wa[:, :, pj * 384:(pj + 1) * 384].reshape(128, 3072)
        w_out_r[l_, :, 0:2048] = w_out[l_, 0:256].reshape(2, 128, D).transpose(1, 0, 2).reshape(128, 2048)
        w_out_r[l_, :, 2048:5120] = w_out[l_, 256:640].reshape(3, 128, D).transpose(1, 0, 2).reshape(128, 3072)
        Wc = w_out[l_, 640:1024].reshape(4, 96, D).transpose(1, 0, 2)
        w_out_r[l_, 0:96, 5120:7168] = Wc[:, :, 0:512].reshape(96, 2048)
        w_out_r[l_, 0:96, 7168:9216] = Wc[:, :, 512:1024].reshape(96, 2048)
    shared = {
        "w_ada_r": w_ada_r,
        "b_ada": np.ascontiguousarray(np.asarray(b_ada, f32).reshape(nl, 24, 128).transpose(0, 2, 1)),
        "norm_w": np.ascontiguousarray(np.asarray(norm_w, f32).reshape(nl, 8, 128).transpose(0, 2, 1)),
        "w_in_r": w_in_r,
        "w_four": np.ascontiguousarray(np.asarray(w_four, f32)),
        "biasT": biasT,
        "walpha": walpha.reshape(nl, 33, 384),
        "gnw": np.ascontiguousarray(np.asarray(gla_norm_w, f32).reshape(nl, 96, 1)),
        "w_out_r": w_out_r,
        "normf": np.ascontiguousarray(np.asarray(norm_f, f32).reshape(8, 128).T),
        "ident": C["ident"], "dft64": C["dft64"], "dftL": C["dftL"], "dftC": C["dftC"],
        "ropeC": C["ropeC"], "ropeS": C["ropeS"], "tri": C["tri"], "mask4": C["mask4"],
    }
    in_maps = []
    for b in cores:
        xT = np.ascontiguousarray(np.concatenate([x[b], ctx[b]], axis=0).T)
        cv = np.stack([c[b].reshape(8, 128).T, c_ctx.reshape(8, 128).T], axis=-1)
        m = dict(shared)
        m["xT"] = xT
        m["cvec"] = np.ascontiguousarray(cv.reshape(128, 16))
        in_maps.append(m)
    return in_maps


def kernel(**inputs):
    if "nc" not in _NC_CACHE:
        _NC_CACHE["nc"] = build()[0]
    nc = _NC_CACHE["nc"]
    in_maps = prep_inputs(cores=list(range(8)), **inputs)
    res = run_bass_kernel_spmd(nc, in_maps, core_ids=list(range(8)))
    out = np.stack([np.asarray(r["outT"]).T for r in res.results], axis=0)
    return np.ascontiguousarray(out.astype(np.float32))
```

```python
import contextlib
import numpy as np
import ml_dtypes
import concourse.bass as bass
import concourse.mybir as mybir
from concourse.bass_utils import run_bass_kernel_spmd

F32 = mybir.dt.float32
BF16 = mybir.dt.bfloat16
AF = mybir.ActivationFunctionType
ALU = mybir.AluOpType
NPBF = ml_dtypes.bfloat16

D = 1024
L = 2048
LC = 256
NTOK = L + LC
NT = NTOK // 128
DIN = 3232
EPS = 1e-6
PAD_BIAS = -100.0
TBS = [(0, 512, 0), (512, 512, 0), (1024, 512, 0), (1536, 512, 0), (2048, 256, 1)]

ENGS = ("sp", "pool", "act", "dve", "pe")


class Buf:
    __slots__ = ("name", "writer", "readers")

    def __init__(self, name=""):
        self.name = name
        self.writer = None
        self.readers = []


class Op:
    __slots__ = ("eng", "fn", "deps", "signal", "sem", "val", "is_dma", "waits", "bar", "phase")

    def __init__(self, eng, fn, is_dma=False, bar=True):
        self.eng = eng
        self.fn = fn
        self.deps = []
        self.signal = False
        self.sem = None
        self.val = 0
        self.is_dma = is_dma
        self.waits = []
        self.bar = bar


class Prog:
    def __init__(self):
        self.ops = []
        self.last = {}
        self.pend = {e: [] for e in ENGS}
        self.bar_dmas = []
        self.phase = ''

    def op(self, eng, fn, reads=(), writes=(), is_dma=False, bar=True):
        o = Op(eng, fn, is_dma, bar)
        o.phase = self.phase
        deps = []
        for b in reads:
            if b.writer is not None:
                deps.append(b.writer)
        for b in writes:
            if b.writer is not None:
                deps.append(b.writer)
            deps.extend(b.readers)
        if self.pend[eng]:
            deps.extend(self.pend[eng])
            self.pend[eng] = []
        seen = set()
        for d in deps:
            if d is o or id(d) in seen:
                continue
            seen.add(id(d))
            if d.eng == "pe" and eng == "pe" and not d.is_dma and not is_dma:
                continue
            o.deps.append(d)
            d.signal = True
        for b in reads:
            if not is_dma:
                b.readers = [r for r in b.readers if r.is_dma or r.eng != eng]
            b.readers.append(o)
        for b in writes:
            b.writer = o
            b.readers = []
        self.ops.append(o)
        if is_dma:
            if bar:
                self.bar_dmas.append(o)
        else:
            self.last[eng] = o
        return o

    def dma(self, q, out, in_, reads=(), writes=(), bar=True):
        return self.op(q, lambda e: e.dma_start(out=out, in_=in_), reads, writes, is_dma=True, bar=bar)

    def barrier(self):
        deps = list(self.last.values()) + self.bar_dmas
        self.bar_dmas = []
        for e in ENGS:
            self.pend[e] = self.pend[e] + deps

    def emit(self, nc, n_hw_sems=16, n_sw_sems=32):
        ops = self.ops
        with contextlib.ExitStack() as es:
            esem = {e: es.enter_context(nc.semaphore("s_" + e)) for e in ENGS}
            hsems = [es.enter_context(nc.semaphore("h%d" % i)) for i in range(n_hw_sems)]
            ssems = [es.enter_context(nc.semaphore("w%d" % i)) for i in range(n_sw_sems)]
            block = es.enter_context(nc.Block())
            pos = {id(o): i for i, o in enumerate(ops)}
            cons = {}
            for o in ops:
                for d in o.deps:
                    if d.is_dma and d.eng == "pool":
                        cons.setdefault(id(d), []).append(o)
            hcount = [0] * n_hw_sems
            hlast = [None] * n_hw_sems
            slast = [None] * n_sw_sems
            sgen = [0] * n_sw_sems
            hi = si = 0
            clear_before = {}
            gen_of = {}
            all_dma = []
            for o in ops:
                if not o.is_dma:
                    continue
                o.signal = True
                all_dma.append(o)
                if o.eng == "pool":
                    k = si % n_sw_sems
                    si += 1
                    prev = slast[k]
                    if prev is not None and all(d is not prev for d in o.deps):
                        o.deps.append(prev)
                    sgen[k] += 16
                    o.sem = ssems[k]
                    o.val = sgen[k]
                    slast[k] = o
                else:
                    k = hi % n_hw_sems
                    hi += 1
                    if hlast[k] is not None and all(d is not hlast[k] for d in o.deps):
                        o.deps.append(hlast[k])
                    hcount[k] += 16
                    o.sem = hsems[k]
                    o.val = hcount[k]
                    hlast[k] = o
            cnt = {e: 0 for e in ENGS}
            for o in ops:
                if o.signal and not o.is_dma:
                    cnt[o.eng] += 1
                    o.sem = esem[o.eng]
                    o.val = cnt[o.eng]
            seen = {e: {} for e in ENGS}
            for o in ops:
                need = {}
                for d in o.deps:
                    key = id(d.sem)
                    if key not in need or need[key][1] < d.val:
                        need[key] = (d.sem, d.val)
                for key, (sem, val) in need.items():
                    if seen[o.eng].get(key, 0) >= val:
                        continue
                    seen[o.eng][key] = val
                    o.waits.append((sem, val))
            tail = {}
            for o in all_dma:
                tail[id(o.sem)] = (o.sem, o.val)

            def run(engname, eng):
                for o in ops:
                    if o.eng != engname:
                        continue
                    for sem, val in o.waits:
                        eng.wait_ge(sem, val)
                    if id(o) in clear_before:
                        eng.sem_clear(clear_before[id(o)])
                    ins = o.fn(eng)
                    if o.signal:
                        ins.then_inc(o.sem, 16 if o.is_dma else 1)
                if engname == "sp":
                    for sem, val in tail.values():
                        eng.wait_ge(sem, val)

            @block.sync
            def _(e):
                run("sp", e)

            @block.gpsimd
            def _(e):
                run("pool", e)

            @block.scalar
            def _(e):
                run("act", e)

            @block.vector
            def _(e):
                run("dve", e)

            @block.tensor
            def _(e):
                run("pe", e)
        return {e: sum(1 for o in ops if o.eng == e) for e in ENGS}, dict(cnt)


def _na_tiles():
    return ([(5, 5 + dk) for dk in (-2, -1, 0, 1, 2)] + [(0, k) for k in range(4)] + [(1, k) for k in range(4)]
            + [(14, k) for k in (12, 13, 14, 15)] + [(15, k) for k in (12, 13, 14, 15)])


def _na_keytiles(j):
    if j == 0:
        return [0, 1, 2, 3], 5
    if j == 1:
        return [0, 1, 2, 3], 9
    if j == 14:
        return [12, 13, 14, 15], 13
    if j == 15:
        return [12, 13, 14, 15], 17
    return [j - 2, j - 1, j, j + 1, j + 2], 0


def _na_bias_index():
    rows, kh = 32, 8
    rs = np.clip(np.arange(rows) - 4, 0, rows - kh)
    cs = np.clip(np.arange(64) - 8, 0, 48)
    tiles = _na_tiles()
    pad = 15 * 31
    idx = np.full((len(tiles), 128, 128), pad, np.int64)
    kc = np.arange(64)[:, None]
    qc = np.arange(64)[None, :]
    co = np.clip(kc - qc + 15, 0, 30)
    valid = (kc >= cs[qc]) & (kc < cs[qc] + 16)
    for t, (j, kt) in enumerate(tiles):
        for a in range(2):
            kr = 2 * kt + a
            for c in range(2):
                r = 2 * j + c
                if not (rs[r] <= kr < rs[r] + kh):
                    continue
                ro = kr - r + 7
                idx[t, a * 64:(a + 1) * 64, c * 64:(c + 1) * 64] = np.where(valid, ro * 31 + co, pad)
    return idx


def _win_pieces():
    ps = [((256, 256),), ((0, 256),), ((1664, 384),)]
    ps += [((512 + 128 * i, 128), (896 + 128 * i, 128), (1280 + 128 * i, 128)) for i in range(3)]
    ps += [((2816, 384),), ((2048, 384),), ((2432, 384),), ((3200, 32),)]
    lay, off = {}, 0
    for p in ps:
        n = sum(k for _, k in p)
        lay[p] = (off, n)
        off += 8 * n
    assert off == 8 * DIN
    return ps, lay


WOUT_BASE = {0: 0, 256: 2048, 640: 5120}

_CONST_CACHE = {}


def _consts():
    if _CONST_CACHE:
        return _CONST_CACHE
    C = _CONST_CACHE
    C["ident"] = np.eye(128, dtype=np.float32).astype(NPBF)
    k = np.arange(64)
    ang = 2 * np.pi * np.outer(k, k) / 64.0
    c64 = np.cos(ang) / 8.0
    s64 = -np.sin(ang) / 8.0
    blk = np.zeros((128, 256), np.float64)
    for g in range(2):
        blk[g * 64:(g + 1) * 64, g * 64:(g + 1) * 64] = c64
        blk[g * 64:(g + 1) * 64, 128 + g * 64:128 + (g + 1) * 64] = s64
    C["dft64"] = blk.astype(np.float32).astype(NPBF)

    def dftn(n):
        t = np.arange(n, dtype=np.int64)
        m = np.outer(t, t) % n
        a = 2 * np.pi * m.astype(np.float64) / n
        return np.stack([np.cos(a), np.sin(a)]).astype(np.float32) / np.float32(np.sqrt(n))
    C["dftL"] = dftn(L).astype(NPBF)
    dc = dftn(LC).astype(NPBF)
    C["dftC"] = np.ascontiguousarray(dc.reshape(2, 2, 128, 256).transpose(2, 1, 0, 3))
    pos = np.arange(L)
    row = (pos // 64).astype(np.float32)
    col = (pos % 64).astype(np.float32)
    inv = (10000.0 ** (-np.arange(12, dtype=np.float32) / 12)).astype(np.float32)
    ar = row[:, None] * inv[None, :]
    ac = col[:, None] * inv[None, :]
    cos48 = np.concatenate([np.cos(ar), np.cos(ar), np.cos(ac), np.cos(ac)], -1)
    sin48 = np.concatenate([-np.sin(ar), np.sin(ar), -np.sin(ac), np.sin(ac)], -1)
    C["ropeC"] = np.ascontiguousarray(cos48.reshape(16, 128, 48).transpose(1, 0, 2)).astype(np.float32)
    C["ropeS"] = np.ascontiguousarray(sin48.reshape(16, 128, 48).transpose(1, 0, 2)).astype(np.float32)
    s = np.arange(128)[:, None]
    t = np.arange(128)[None, :]
    tri = np.stack([(s <= t), (s >= t)]).astype(np.float32)
    C["tri"] = np.ascontiguousarray(tri.transpose(1, 0, 2))
    m4 = np.repeat(tri[:, :, None, :], 4, axis=2)
    C["mask4"] = np.ascontiguousarray(m4.transpose(1, 0, 2, 3)).astype(NPBF)
    C["naidx"] = _na_bias_index()
    return C


def build(n_layers=4, taps=(), skip=()):
    nc = bass.Bass("TRN2", target_bir_lowering=False)

    def din(name, shape, dt=F32):
        return nc.dram_tensor(name, list(shape), dt, kind="ExternalInput").ap()

    xT_d = din("xT", [D, NTOK])
    cvec_d = din("cvec", [128, 16])
    w_ada_d = din("w_ada_r", [4, 128, 8 * 3 * D])
    b_ada_d = din("b_ada", [4, 128, 24])
    norm_w_d = din("norm_w", [4, 128, 8])
    w_in_d = din("w_in_r", [4, 128, 8 * DIN])
    WIN_LAY = _win_pieces()[1]
    w_four_d = din("w_four", [4, 256, 256])
    biasT_d = din("biasT", [4, 6, 128, 21, 128])
    walpha_d = din("walpha", [4, 33, 384])
    gnw_d = din("gnw", [4, 96, 1])
    w_out_d = din("w_out_r", [4, 128, 9216])
    normf_d = din("normf", [128, 8])
    ident_d = din("ident", [128, 128], BF16)
    dft64_d = din("dft64", [128, 256], BF16)
    dftL_d = din("dftL", [2, L, L], BF16)
    dftC_d = din("dftC", [128, 2, 2, 256], BF16)
    ropeC_d = din("ropeC", [128, 16, 48])
    ropeS_d = din("ropeS", [128, 16, 48])
    tri_d = din("tri", [128, 2, 128])
    mask4_d = din("mask4", [128, 2, 4, 128], BF16)
    out_d = nc.dram_tensor("outT", [D, L], F32, kind="ExternalOutput").ap()
    tap_d = {}

    P = Prog()
    with contextlib.ExitStack() as es:
        def sb(name, shape, dt):
            return es.enter_context(nc.sbuf_tensor("sb_" + name, list(shape), dt))

        xT = sb("xT", [128, 8, NTOK], F32)
        hT = sb("hT", [128, 8, NTOK], BF16)
        mixg = sb("mixg", [128, 4, NTOK], BF16)
        WSL = 3072
        wsl = [sb("wsl%d" % i, [128, WSL], BF16) for i in range(2)]
        ARENA = 54 * 1024
        arena = sb("arena", [128, ARENA // 2], BF16)
        ident = sb("ident", [128, 128], BF16)
        ones_bf = sb("ones_bf", [128, 128], BF16)
        ones_f = sb("ones_f", [128, 2], F32)
        dft64 = sb("dft64", [128, 256], BF16)
        dftC = sb("dftC", [128, 2, 2, 256], BF16)
        ropeC = sb("ropeC", [128, 16, 48], F32)
        ropeS = sb("ropeS", [128, 16, 48], F32)
        tri = sb("tri", [128, 2, 128], F32)
        mask4 = sb("mask4", [128, 2, 4, 128], BF16)
        cvec = sb("cvec", [128, 8, 2], F32)
        sc_bf = sb("sc_bf", [128, 8, 2], BF16)
        mod_sbs = [sb("mod_sb%d" % i, [128, 24, 2], F32) for i in range(2)]
        Gms = [sb("Gm%d" % i, [128, 8, 2], F32) for i in range(2)]
        b_adas = [sb("b_ada%d" % i, [128, 24], F32) for i in range(2)]
        norm_ws = [sb("norm_w%d" % i, [128, 8], F32) for i in range(2)]
        mod_sb, Gm = mod_sbs[0], Gms[0]
        normf = sb("normf", [128, 8], F32)
        gnw = sb("gnw", [96, 1], F32)
        walpha = sb("walpha", [33, 2, 192], F32)
        wfour = sb("wfour", [128, 2, 256], BF16)
        pbank = [es.enter_context(nc.psum_tensor("pb%d" % i, [128, 512], F32)) for i in range(8)]
        Bp = [Buf("pb%d" % i) for i in range(8)]

        def carve(off, nelem, dt, parts=128):
            nb = nelem * (4 if dt == F32 else 2)
            assert off % 4 == 0 and off + nb <= ARENA, (off, nb, ARENA)
            v = arena[0:parts, off // 2: (off + nb) // 2]
            return v.bitcast(F32) if dt == F32 else v

        Bx = [[Buf("x%d_%d" % (c, t)) for t in range(5)] for c in range(8)]
        Bh = [[Buf("h%d_%d" % (c, t)) for t in range(5)] for c in range(8)]
        Bm = [[Buf("m%d_%d" % (c, t)) for t in range(5)] for c in range(4)]
        Bw = [Buf("wsl0"), Buf("wsl1")]
        Bc = Buf("consts")
        Bmod = Buf("mod")
        state = {"ws": 0, "pb": 0}

        def next_slot():
            k = state["ws"]
            state["ws"] = 1 - k
            return wsl[k], Bw[k]

        def next_bank():
            k = state["pb"]
            state["pb"] = (k + 1) % 8
            return pbank[k], Bp[k]

        def mm(out, lhsT, rhs, start, stop, reads, writes):
            P.op("pe", lambda e: e.matmul(out, lhsT=lhsT, rhs=rhs, start=start, stop=stop), reads, writes)

        def act(out, in_, func, reads, writes, bias=None, scale=None):
            kw = {}
            if bias is not None:
                kw["bias"] = bias
            if scale is not None:
                kw["scale"] = scale
            P.op("act", lambda e: e.activation(out=out, in_=in_, func=func, **kw), reads, writes)

        def vtt(out, in0, in1, op, reads, writes, eng="dve"):
            P.op(eng, lambda e: e.tensor_tensor(out=out, in0=in0, in1=in1, op=op), reads, writes)

        def vts(out, in0, s1, op0, reads, writes, s2=None, op1=None, eng="dve"):
            if op1 is None:
                P.op(eng, lambda e: e.tensor_scalar(out=out, in0=in0, scalar1=s1, scalar2=None, op0=op0), reads, writes)
            else:
                P.op(eng, lambda e: e.tensor_scalar(out=out, in0=in0, scalar1=s1, scalar2=s2, op0=op0, op1=op1), reads, writes)

        def vstt(out, in0, scalar, in1, op0, op1, reads, writes):
            P.op("dve", lambda e: e.scalar_tensor_tensor(out=out, in0=in0, scalar=scalar, in1=in1, op0=op0, op1=op1), reads, writes)

        def vcopy(out, in_, reads, writes, eng="dve"):
            P.op(eng, lambda e: e.tensor_copy(out=out, in_=in_), reads, writes)

        def vrecip(out, in_, reads, writes):
            P.op("dve", lambda e: e.reciprocal(out=out, in_=in_), reads, writes)

        def memset(ap, val, writes, eng="pool"):
            P.op(eng, lambda e: e.memset(ap, val), (), writes)

        def tap(name, ap, shape, reads, dt=F32):
            if name in taps:
                t = nc.dram_tensor("tap_" + name, list(shape), dt, kind="ExternalOutput").ap()
                tap_d[name] = t
                P.dma("sp", t, ap, reads=reads)

        for c in range(8):
            P.dma("sp", xT[:, c, :], xT_d[c * 128:(c + 1) * 128, :], writes=[Bx[c][t] for t in range(5)])
        for dst, src in ((ident, ident_d), (dft64, dft64_d), (dftC, dftC_d), (ropeC, ropeC_d), (ropeS, ropeS_d),
                         (tri, tri_d), (mask4, mask4_d), (normf, normf_d)):
            P.dma("sp", dst[:], src, writes=[Bc])
        P.dma("sp", cvec[:, :, :].rearrange("p c v -> p (c v)"), cvec_d, writes=[Bc])
        memset(ones_bf[:], 1.0, [Bc])
        memset(ones_f[:], 1.0, [Bc])
        act(sc_bf[:, :, :], cvec[:, :, :], AF.Silu, [Bc], [Bc])

        def load_w(l, pieces):
            slot, bw = next_slot()
            off, ntot = WIN_LAY[tuple(pieces)]
            assert 8 * ntot <= WSL
            P.dma("pool", slot[:, 0:8 * ntot], w_in_d[l][:, off:off + 8 * ntot], writes=[bw], bar=False)
            return slot[:, 0:8 * ntot].rearrange("p (c n) -> p c n", c=8), bw

        def proj_fm(wv, bw, col0, ncols, ti, ps, bps, pparts=None):
            t0, tw, _ = TBS[ti]
            for c in range(8):
                mm(ps[0:ncols, 0:tw], wv[:, c, col0:col0 + ncols], hT[:, c, t0:t0 + tw], c == 0, c == 7,
                   [bw, Bh[c][ti]], [bps])

        def proj_tm(wv, bw, col0, ncols, j, ps, bps):
            ti = min(j // 4, 4)
            for c in range(8):
                mm(ps[:, 0:ncols], hT[:, c, j * 128:(j + 1) * 128], wv[:, c, col0:col0 + ncols], c == 0, c == 7,
                   [bw, Bh[c][ti]], [bps])

        def out_proj(l, row0, kparts, nk, srcs, tis):
            mh = 8 if nk * 1024 <= WSL else 4
            for m0 in range(0, 8, mh):
                slot, bw = next_slot()
                wv = slot[0:kparts, 0:nk * mh * 128].rearrange("p (k n) -> p k n", k=nk)
                woff = WOUT_BASE[row0] + (m0 // mh) * nk * mh * 128
                P.dma("pool", slot[0:kparts, 0:nk * mh * 128], w_out_d[l][0:kparts, woff:woff + nk * mh * 128], writes=[bw], bar=False)
                for ti in tis:
                    t0, tw, v = TBS[ti]
                    for m in range(m0, m0 + mh):
                        ps, bps = next_bank()
                        for k in range(nk):
                            mm(ps[:, 0:tw], wv[:, k, (m - m0) * 128:(m - m0 + 1) * 128], mixg[0:kparts, k, t0:t0 + tw],
                               k == 0, k == nk - 1, [bw, Bm[k][ti]], [bps])
                        vstt(xT[:, m, t0:t0 + tw], ps[:, 0:tw], mod_sb[:, 16 + m, v:v + 1], xT[:, m, t0:t0 + tw],
                             ALU.mult, ALU.add, [bps, Bmod, Bx[m][ti]], [Bx[m][ti]])

        def ada_gen(l):
            msb, gm, ba, nw = mod_sbs[l % 2], Gms[l % 2], b_adas[l % 2], norm_ws[l % 2]
            P.dma("sp", ba[:], b_ada_d[l], writes=[Bmod])
            P.dma("sp", nw[:], norm_w_d[l], writes=[Bmod])
            for pj in range(8):
                ph = P.phase
                P.phase = 'L%d.ada' % l
                slot, bw = next_slot()
                wv = slot[:, 0:8 * 384].rearrange("p (c n) -> p c n", c=8)
                P.dma("pool", slot[:, 0:8 * 384], w_ada_d[l][:, pj * 3072:(pj + 1) * 3072], writes=[bw], bar=False)
                ps, bps = next_bank()
                pv = ps[:, 0:6].rearrange("p (j v) -> p j v", v=2)
                for j3 in range(3):
                    for c in range(8):
                        mm(pv[:, j3, :], wv[:, c, j3 * 128:(j3 + 1) * 128], sc_bf[:, c, :], c == 0, c == 7, [bw, Bc], [bps])
                for v in range(2):
                    vtt(msb[:, pj * 3:(pj + 1) * 3, v], pv[:, :, v], ba[:, pj * 3:(pj + 1) * 3], ALU.add, [bps, Bmod], [Bmod])
                P.phase = ph
                yield
            for v in range(2):
                vstt(gm[:, :, v], msb[:, 8:16, v], 1.0, nw[:, :], ALU.add, ALU.mult, [Bmod], [Bmod])
            yield

        for l in range(n_layers):
            last = (l == 3)
            tis_all = [0, 1, 2, 3, 4]
            tis_out = [0, 1, 2, 3] if last else tis_all

            P.phase = 'L%d.ada' % l
            P.dma("sp", gnw[:], gnw_d[l], writes=[Bmod])
            P.dma("sp", walpha[:, :, :].rearrange("p a b -> p (a b)"), walpha_d[l], writes=[Bmod])
            P.dma("pool", wfour[:, :, :], w_four_d[l].rearrange("(k p) n -> p k n", p=128), writes=[Bmod])
            if l == 0 or "B" in skip:
                for _ in ada_gen(l):
                    pass
            mod_sb, Gm = mod_sbs[l % 2], Gms[l % 2]

            if l > 0:
                P.barrier()
            P.phase = 'L%d.norm' % l
            sqs = [carve(8192 * i, 8 * 512, BF16).rearrange("p (c n) -> p c n", c=8) for i in range(2)]
            rstds = [carve(16384 + 2048 * i, 512, F32) for i in range(2)]
            tmpn = [carve(20480 + 2048 * i, 512, F32) for i in range(4)]
            Bsqs = [[Buf() for _ in range(8)] for _ in range(2)]
            Brss, Btn = [Buf(), Buf()], [Buf() for _ in range(4)]

            def n_stat(ti):
                t0, tw, v = TBS[ti]
                sq, bsq = sqs[ti % 2], Bsqs[ti % 2]
                for c in range(8):
                    if c % 4 != 3:
                        act(sq[:, c, 0:tw], xT[:, c, t0:t0 + tw], AF.Square, [Bx[c][ti]], [bsq[c]])
                    else:
                        vtt(sq[:, c, 0:tw], xT[:, c, t0:t0 + tw], xT[:, c, t0:t0 + tw], ALU.mult, [Bx[c][ti]], [bsq[c]])
                ps, bps = next_bank()
                for c in range(8):
                    mm(ps[:, 0:tw], ones_bf[:, :], sq[:, c, 0:tw], c == 0, c == 7, [bsq[c], Bc], [bps])
                act(rstds[ti % 2][:, 0:tw], ps[:, 0:tw], AF.Sqrt, [bps], [Brss[ti % 2]], bias=EPS, scale=1.0 / D)
                vrecip(rstds[ti % 2][:, 0:tw], rstds[ti % 2][:, 0:tw], [Brss[ti % 2]], [Brss[ti % 2]])

            def n_mod(ti):
                t0, tw, v = TBS[ti]
                rstd, brs = rstds[ti % 2], Brss[ti % 2]
                for c in range(8):
                    k = c % 4
                    vtt(tmpn[k][:, 0:tw], xT[:, c, t0:t0 + tw], rstd[:, 0:tw], ALU.mult, [Bx[c][ti], brs], [Btn[k]],
                        eng=("dve" if c % 2 == 0 else "pool"))
                    act(hT[:, c, t0:t0 + tw], tmpn[k][:, 0:tw], AF.Identity, [Btn[k], Bmod], [Bh[c][ti]],
                        bias=mod_sb[:, c, v:v + 1], scale=Gm[:, c, v:v + 1])
            n_stat(0)
            for ti in range(5):
                if ti + 1 < 5:
                    n_stat(ti + 1)
                n_mod(ti)
            if l == 0:
                tap("hT", hT[:, :, :].rearrange("p c n -> p (c n)"), [128, 8 * NTOK], [Bh[c][t] for c in range(8) for t in range(5)], BF16)
            preA = load_w(l, ((256, 256),))
            P.barrier()

            if "A" not in skip:
                P.phase = 'L%d.A' % l
                tis = tis_out
                wv, bw = preA
                for m in range(2):
                    for ti in tis:
                        t0, tw, _ = TBS[ti]
                        ps, bps = next_bank()
                        proj_fm(wv, bw, m * 128, 128, ti, ps, bps)
                        act(mixg[:, m, t0:t0 + tw], ps[:, 0:tw], AF.Silu, [bps], [Bm[m][ti]])
                P.phase = 'L%d.A.u' % l
                uT = carve(0, 2 * NTOK, BF16).rearrange("p (m n) -> p m n", m=2)
                BuT = [[Buf() for _ in range(5)] for _ in range(2)]
                wv, bw = load_w(l, ((0, 256),))
                for m in range(2):
                    for ti in tis:
                        t0, tw, _ = TBS[ti]
                        ps, bps = next_bank()
                        proj_fm(wv, bw, m * 128, 128, ti, ps, bps)
                        vcopy(uT[:, m, t0:t0 + tw], ps[:, 0:tw], [bps], [BuT[m][ti]])
                P.phase = 'L%d.A.ucs' % l
                uCS = carve(9216, NT * 512, BF16).rearrange("p (j n) -> p j n", j=NT)
                BuCS = [Buf() for _ in range(NT)]
                njt = NT if not last else 16
                for j in range(njt):
                    ti = min(j // 4, 4)
                    ps, bps = next_bank()
                    for k in range(2):
                        mm(ps[:, k * 128:(k + 1) * 128], uT[:, k, j * 128:(j + 1) * 128], dft64[:, 0:128], True, True,
                           [BuT[k][ti], Bc], [bps])
                        mm(ps[:, 256 + k * 128:256 + (k + 1) * 128], uT[:, k, j * 128:(j + 1) * 128], dft64[:, 128:256], True, True,
                           [BuT[k][ti], Bc], [bps])
                    if j % 2 == 0:
                        act(uCS[:, j, :], ps[:, :], AF.Copy, [bps], [BuCS[j]])
                    else:
                        vcopy(uCS[:, j, :], ps[:, :], [bps], [BuCS[j]])
                P.phase = 'L%d.A.dft' % l
                dbuf = [carve(27648 + 8192 * i, 2 * L, BF16).rearrange("p (a n) -> p a n", a=2) for i in range(2)]
                Bdb = [Buf(), Buf()]
                for k in range(16):
                    db, bdb = dbuf[k % 2], Bdb[k % 2]
                    P.dma("sp", db[:, 0, :], dftL_d[0, k * 128:(k + 1) * 128, :], writes=[bdb])
                    P.dma("sp", db[:, 1, :], dftL_d[1, k * 128:(k + 1) * 128, :], writes=[bdb])
                    for m in range(2):
                        for n in range(4):
                            b = m * 4 + n
                            mm(pbank[b][:, :], uCS[:, k, m * 128:(m + 1) * 128], db[:, 0, n * 512:(n + 1) * 512], k == 0, False,
                               [BuCS[k], bdb], [Bp[b]])
                            mm(pbank[b][:, :], uCS[:, k, 256 + m * 128:256 + (m + 1) * 128], db[:, 1, n * 512:(n + 1) * 512], False, k == 15,
                               [BuCS[k], bdb], [Bp[b]])
                fT = uT
                BfT = BuT
                for m in range(2):
                    for n in range(4):
                        b = m * 4 + n
                        if b % 2 == 0:
                            act(fT[:, m, n * 512:(n + 1) * 512], pbank[b][:, :], AF.Copy, [Bp[b]] + BuCS, [BfT[m][n]])
                        else:
                            vcopy(fT[:, m, n * 512:(n + 1) * 512], pbank[b][:, :], [Bp[b]] + BuCS, [BfT[m][n]])
                if not last:
                    for m in range(2):
                        ps, bps = next_bank()
                        for k in range(2):
                            mm(ps[:, 0:256], uCS[:, 16 + k, m * 128:(m + 1) * 128], dftC[:, k, 0, :], k == 0, False,
                               [BuCS[16 + k], Bc], [bps])
                            mm(ps[:, 0:256], uCS[:, 16 + k, 256 + m * 128:256 + (m + 1) * 128], dftC[:, k, 1, :], False, k == 1,
                               [BuCS[16 + k], Bc], [bps])
                        vcopy(fT[:, m, L:NTOK], ps[:, 0:256], [bps] + BuCS, [BfT[m][4]])
                P.phase = 'L%d.A.four' % l
                for m2 in range(2):
                    for ti in tis:
                        t0, tw, _ = TBS[ti]
                        ps, bps = next_bank()
                        for m in range(2):
                            mm(ps[:, 0:tw], wfour[:, m, m2 * 128:(m2 + 1) * 128], fT[:, m, t0:t0 + tw], m == 0, m == 1,
                               [Bmod, BfT[m][ti]], [bps])
                        vtt(mixg[:, m2, t0:t0 + tw], ps[:, 0:tw], mixg[:, m2, t0:t0 + tw], ALU.mult, [bps, Bm[m2][ti]], [Bm[m2][ti]])
                if l == 0:
                    tap("mixA", mixg[:, 0:2, :].rearrange("p c n -> p (c n)"), [128, 2 * NTOK], [Bm[c][t] for c in range(2) for t in range(5)], BF16)
                P.phase = 'L%d.A.out' % l
                out_proj(l, 0, 128, 2, None, tis)
                preB = load_w(l, ((1664, 384),))
                P.barrier()

            if "B" not in skip:
                P.phase = 'L%d.B' % l
                tis = tis_out
                wv, bw = preB
                for m in range(3):
                    for ti in tis:
                        t0, tw, _ = TBS[ti]
                        ps, bps = next_bank()
                        proj_fm(wv, bw, m * 128, 128, ti, ps, bps)
                        act(mixg[:, m, t0:t0 + tw], ps[:, 0:tw], AF.Silu, [bps], [Bm[m][ti]])
                qT = carve(0, NTOK, BF16)
                kT = carve(4608, NTOK, BF16)
                vtk = carve(9216, NT * 130, BF16).rearrange("p (j h d) -> p j h d", j=NT, h=2)
                EB = carve(13896, 2 * 21 * 128, BF16).rearrange("p (h t q) -> p h t q", h=2, t=21)
                stage = carve(24648, 21 * 128, F32).rearrange("p (t q) -> p t q", t=21)
                NE = 6
                Et = [carve(35400 + 1792 * i, 7 * 128, BF16).rearrange("p (t q) -> p t q", t=7) for i in range(NE)]
                osb = [carve(46152 + 256 * i, 128, BF16) for i in range(2)]
                rden = [carve(46664 + 8 * i, 2, F32) for i in range(2)]
                Bq = [Buf() for _ in range(5)]
                Bk = [Buf() for _ in range(5)]
                Bv = [Buf() for _ in range(NT)]
                BEB, Bst = [Buf(), Buf()], Buf()
                BE = [Buf() for _ in range(NE)]
                Bos, Brd = [Buf(), Buf()], [Buf(), Buf()]
                for j in range(NT):
                    memset(vtk[:, j, :, 64:65], 1.0, [Bv[j]])
                P.phase = 'L%d.B.proj' % l
                adag = ada_gen(l + 1) if l + 1 < n_layers else None
                for i in range(3):
                    wv, bw = load_w(l, ((512 + 128 * i, 128), (896 + 128 * i, 128), (1280 + 128 * i, 128),))
                    for ti in tis_all:
                        t0, tw, _ = TBS[ti]
                        if ti in tis:
                            ps, bps = next_bank()
                            proj_fm(wv, bw, 0, 128, ti, ps, bps)
                            act(qT[:, t0:t0 + tw], ps[:, 0:tw], AF.Copy, [bps], [Bq[ti]])
                        ps, bps = next_bank()
                        proj_fm(wv, bw, 128, 128, ti, ps, bps)
                        vcopy(kT[:, t0:t0 + tw], ps[:, 0:tw], [bps], [Bk[ti]])
                    for j in range(NT):
                        ps, bps = next_bank()
                        proj_tm(wv, bw, 256, 128, j, ps, bps)
                        src = ps[:, 0:128].rearrange("p (h d) -> p h d", h=2)
                        if j % 2 == 0:
                            act(vtk[:, j, :, 0:64], src, AF.Copy, [bps], [Bv[j]])
                        else:
                            vcopy(vtk[:, j, :, 0:64], src, [bps], [Bv[j]])
                    P.phase = 'L%d.B.proj' % l
                    for h2 in range(2):
                        P.dma("sp", stage[:, :, :], biasT_d[l, 2 * i + h2], writes=[Bst])
                        act(EB[:, h2, :, :], stage[:, :, :], AF.Exp, [Bst], [BEB[h2]])
                    P.phase = 'L%d.B.att' % l
                    items = []
                    for j in range(16):
                        kts, e0 = _na_keytiles(j)
                        items.append((j, kts, e0, True))
                    if not last:
                        for j in (16, 17):
                            items.append((j, [], 0, False))
                    work = [(it, h2) for it in items for h2 in range(2)]

                    def stage1(n):
                        (j, kts, e0, loc), h2 = work[n]
                        base = 64 * h2
                        E, bE = Et[n % NE], BE[n % NE]
                        allk = kts + [16, 17]
                        nl = len(kts)
                        tq = min(j // 4, 4)
                        psA, bA = next_bank()
                        psB, bB = (None, None)
                        if len(allk) > 4:
                            psB, bB = next_bank()
                        for t, kt in enumerate(allk):
                            ps, bps = (psA, bA) if t < 4 else (psB, bB)
                            tt = t % 4
                            mm(ps[:, tt * 128:(tt + 1) * 128], kT[base:base + 64, kt * 128:(kt + 1) * 128],
                               qT[base:base + 64, j * 128:(j + 1) * 128], True, True, [Bk[min(kt // 4, 4)], Bq[tq]], [bps])
                        na = min(4, len(allk))
                        act(E[:, 0:na, :], psA[:, 0:na * 128].rearrange("p (t q) -> p t q", t=na), AF.Exp, [bA], [bE], scale=0.125)
                        if len(allk) > 4:
                            nb = len(allk) - 4
                            act(E[:, 4:4 + nb, :], psB[:, 0:nb * 128].rearrange("p (t q) -> p t q", t=nb), AF.Exp, [bB], [bE], scale=0.125)
                        if nl:
                            vtt(E[:, 0:nl, :], E[:, 0:nl, :], EB[:, h2, e0:e0 + nl, :], ALU.mult, [bE, BEB[h2]], [bE])
                        return allk

                    def stage2(n, allk, ops_, bops):
                        (j, kts, e0, loc), h2 = work[n]
                        E, bE = Et[n % NE], BE[n % NE]
                        for t, kt in enumerate(allk):
                            mm(ops_[:, h2 * 128:h2 * 128 + 65], E[:, t, :], vtk[:, kt, h2, :], t == 0, t == len(allk) - 1,
                               [bE, Bv[kt]], [bops])
                        if h2 == 1:
                            pr = (n // 2) % 2
                            ov = ops_[:, 0:256].rearrange("p (h d) -> p h d", h=2)
                            vrecip(rden[pr][:, :], ov[:, :, 64], [bops], [Brd[pr]])
                            for hh in range(2):
                                vts(osb[pr][:, hh * 64:(hh + 1) * 64], ov[:, hh, 0:64], rden[pr][:, hh:hh + 1], ALU.mult,
                                    [bops, Brd[pr]], [Bos[pr]])
                            pt, bpt = next_bank()
                            ptb = pt[:, 0:64].bitcast(BF16)
                            P.op("pe", lambda e: e.transpose(out=ptb, in_=osb[pr][:, :], identity=ident[:, :]), [Bos[pr], Bc], [bpt])
                            tq = min(j // 4, 4)
                            vtt(mixg[:, i, j * 128:(j + 1) * 128], ptb, mixg[:, i, j * 128:(j + 1) * 128], ALU.mult,
                                [bpt, Bm[i][tq]], [Bm[i][tq]])

                    DEPTH = 3
                    allks = {}
                    cur_o = None
                    for n in range(len(work) + DEPTH):
                        if adag is not None and n % 24 == 12:
                            next(adag, None)
                        if n < len(work):
                            allks[n] = stage1(n)
                        pn = n - DEPTH
                        if pn >= 0:
                            if work[pn][1] == 0:
                                cur_o = next_bank()
                            stage2(pn, allks.pop(pn), cur_o[0], cur_o[1])
                if l == 0:
                    tap("mixB", mixg[:, 0:3, :].rearrange("p c n -> p (c n)"), [128, 3 * NTOK], [Bm[c][t] for c in range(3) for t in range(5)], BF16)
                if adag is not None:
                    for _ in adag:
                        pass
                P.phase = 'L%d.B.out' % l
                out_proj(l, 256, 128, 3, None, tis)
                preC = load_w(l, ((2816, 384),))
                P.barrier()

            if "C" not in skip:
                P.phase = 'L%d.C' % l
                tis = tis_out
                wv, bw = preC
                for h in range(4):
                    for ti in tis:
                        t0, tw, _ = TBS[ti]
                        ps, bps = next_bank()
                        proj_fm(wv, bw, h * 96, 96, ti, ps, bps)
                        act(mixg[0:96, h, t0:t0 + tw], ps[0:96, 0:tw], AF.Silu, [bps], [Bm[h][ti]])
                qkr = carve(0, NT * 384, BF16).rearrange("p (j g d) -> p j g d", j=NT, g=8)
                vg = carve(13824, NT * 384, BF16).rearrange("p (j n) -> p j n", j=NT)
                zT = carve(27648, NTOK, F32, parts=33)
                off = 36864
                Bqk = [Buf() for _ in range(NT)]
                Bvg = [Buf() for _ in range(NT)]
                Bz = [Buf() for _ in range(5)]

                def cv(n, dt, parts=128):
                    nonlocal off
                    v = carve(off, n, dt, parts)
                    off += n * (4 if dt == F32 else 2)
                    off = (off + 3) // 4 * 4
                    return v
                t1 = cv(384, F32)
                t2 = cv(384, F32)
                Bt1, Bt2 = Buf(), Buf()
                P.phase = 'L%d.C.proj' % l
                wv, bw = load_w(l, ((2048, 384),))
                for j in range(NT):
                    ps, bps = next_bank()
                    proj_tm(wv, bw, 0, 384, j, ps, bps)
                    if j < 16:
                        cb = ropeC[:, j, :].unsqueeze(1).to_broadcast([128, 8, 48])
                        vtt(t1[:, :].rearrange("p (g d) -> p g d", g=8), ps[:, 0:384].rearrange("p (g d) -> p g d", g=8), cb,
                            ALU.mult, [bps, Bc], [Bt1])
                        psv = ps[:, 0:384].rearrange("p (g f u w) -> p g f u w", g=8, f=2, u=2)
                        t2v = t2[:, :].rearrange("p (g f u w) -> p g f u w", g=8, f=2, u=2)
                        sv = ropeS[:, j, :].rearrange("p (f u w) -> p f u w", f=2, u=2)
                        for u in range(2):
                            vtt(t2v[:, :, :, u, :], psv[:, :, :, 1 - u, :], sv[:, :, u, :].unsqueeze(1).to_broadcast([128, 8, 2, 12]),
                                ALU.mult, [bps, Bc], [Bt2])
                        vtt(qkr[:, j, :, :], t1[:, :].rearrange("p (g d) -> p g d", g=8), t2[:, :].rearrange("p (g d) -> p g d", g=8),
                            ALU.add, [Bt1, Bt2], [Bqk[j]])
                    else:
                        vcopy(qkr[:, j, :, :], ps[:, 0:384].rearrange("p (g d) -> p g d", g=8), [bps], [Bqk[j]])
                wv, bw = load_w(l, ((2432, 384),))
                for j in range(NT):
                    ps, bps = next_bank()
                    proj_tm(wv, bw, 0, 384, j, ps, bps)
                    if j % 2 == 0:
                        act(vg[:, j, :], ps[:, 0:384], AF.Copy, [bps], [Bvg[j]])
                    else:
                        vcopy(vg[:, j, :], ps[:, 0:384], [bps], [Bvg[j]])
                wv, bw = load_w(l, ((3200, 32),))
                memset(zT[32:33, :], 1.0, Bz)
                for ti in tis_all:
                    t0, tw, _ = TBS[ti]
                    ps, bps = next_bank()
                    proj_fm(wv, bw, 0, 32, ti, ps, bps)
                    vcopy(zT[0:32, t0:t0 + tw], ps[0:32, 0:tw], [bps], [Bz[ti]])
                P.barrier()
                oacc = hT[0:96, :, :].rearrange("p c n -> p (c n)").bitcast(F32).rearrange("p (h n) -> p h n", h=4)
                Boa = [Buf() for _ in range(NT)]
                class _Al:
                    def __init__(self, ap2d, nbytes, start=0):
                        self.ap, self.n, self.off = ap2d, nbytes, start

                    def __call__(self, nelem, dt, parts=128):
                        nb = nelem * (4 if dt == F32 else 2)
                        assert self.off % 4 == 0 and self.off + nb <= self.n, (self.off, nb, self.n)
                        v = self.ap[0:parts, self.off // 2:(self.off + nb) // 2]
                        self.off = (self.off + nb + 3) // 4 * 4
                        return v.bitcast(F32) if dt == F32 else v
                tot = carve(36864, 512, F32, 96)
                sqo = carve(36864 + 2048, 512, BF16, 96)
                al0 = _Al(arena, ARENA, 39936)
                alA, alB = _Al(wsl[0], 2 * WSL), _Al(wsl[1], 2 * WSL)
                Btot, Bsqo = Buf(), Buf()
                LNS = float(np.log(48.0 ** -0.5))
                SC = []
                for d in range(2):
                    a1, a2 = (al0, al0) if d == 0 else (alA, alB)
                    sc = {}
                    sc["spd"] = [a1(256, F32) for _ in range(2)]
                    sc["eq"] = [a1(192, F32)]
                    sc["ek"] = [a1(192, F32)]
                    sc["qd"] = [a1(256, BF16) for _ in range(2)]
                    sc["kd"] = [a1(256, BF16) for _ in range(3)]
                    sc["qkT"] = [a2(512, BF16) for _ in range(2)]
                    sc["Am"] = [a2(512, BF16) for _ in range(2)]
                    sc["ebend"] = [a2(2, F32) for _ in range(3)]
                    sc["st_f"] = a2(192, F32)
                    sc["st_t"] = a2(192, F32)
                    sc["st_b"] = a2(192, BF16)
                    sc["B"] = {k: [Buf() for _ in v] for k, v in sc.items() if isinstance(v, list)}
                    sc["Bst_f"], sc["Bst_t"], sc["Bst_b"] = Buf(), Buf(), Buf()
                    for k in ("spd", "qd", "kd"):
                        for i_, t_ in enumerate(sc[k]):
                            memset(t_[:, :], 0.0, [sc["B"][k][i_]])
                    memset(sc["st_f"][:, :], 0.0, [sc["Bst_f"]])
                    memset(sc["st_b"][:, :], 0.0, [sc["Bst_b"]])
                    sc["order"] = [16, 17] + list(range(16)) if d == 0 else [17, 16] + list(range(15, -1, -1))
                    SC.append(sc)
                P.phase = 'L%d.C.sweep' % l
                stored = set()

                def h4(ap):
                    return ap.rearrange("p (h d) -> p h d", h=4)

                def st_a(d, n):
                    sc = SC[d]; c = sc["order"][n]; ti = min(c // 4, 4)
                    spd, bspd = sc["spd"][n % 2], sc["B"]["spd"][n % 2]
                    gps, bg = next_bank()
                    mm(gps[:, 0:192], zT[0:33, c * 128:(c + 1) * 128], walpha[0:33, d, :], True, True, [Bz[ti], Bmod], [bg])
                    spv = h4(spd[:, :])[:, :, 0:48]
                    act(spv, h4(gps[:, 0:192]), AF.Exp, [bg], [bspd], scale=-1.0)
                    act(spv, spv, AF.Ln, [bspd], [bspd], bias=1.0)

                def st_b(d, n):
                    sc = SC[d]; c = sc["order"][n]
                    spd, bspd = sc["spd"][n % 2], sc["B"]["spd"][n % 2]
                    eq, beq = sc["eq"][0], sc["B"]["eq"][0]
                    ek, bek = sc["ek"][0], sc["B"]["ek"][0]
                    qd, bqd = sc["qd"][n % 2], sc["B"]["qd"][n % 2]
                    kd, bkd = sc["kd"][n % 3], sc["B"]["kd"][n % 3]
                    eb, beb = sc["ebend"][n % 3], sc["B"]["ebend"][n % 3]
                    bps_, bb = next_bank()
                    mm(bps_[:, 0:256], tri[:, d, :], spd[:, :], True, True, [Bc, bspd], [bb])
                    for i in range(2):
                        mm(bps_[:, 256 + 2 * i:258 + 2 * i], spd[:, i * 128:(i + 1) * 128], ones_f[:, 0:2], True, True, [bspd, Bc], [bb])
                    bv = h4(bps_[:, 0:256])[:, :, 0:48]
                    act(h4(eq[:, :]), bv, AF.Exp, [bb], [beq], bias=LNS, scale=-1.0 / 16)
                    act(h4(ek[:, :]), bv, AF.Exp, [bb], [bek], scale=1.0 / 16)
                    act(eb[:, :], bps_[:, 256:260].rearrange("p (i two) -> p i two", two=2)[:, :, 0], AF.Exp, [bb], [beb], scale=-1.0 / 16)
                    vtt(h4(qd[:, :])[:, :, 0:48], qkr[:, c, 0:4, :], h4(eq[:, :]), ALU.mult, [Bqk[c], beq], [bqd])
                    vtt(h4(kd[:, :])[:, :, 0:48], qkr[:, c, 4:8, :], h4(ek[:, :]), ALU.mult, [Bqk[c], bek], [bkd])

                def st_c(d, n):
                    sc = SC[d]
                    qd, bqd = sc["qd"][n % 2], sc["B"]["qd"][n % 2]
                    kd, bkd = sc["kd"][n % 3], sc["B"]["kd"][n % 3]
                    qkT, bqkT = sc["qkT"][n % 2], sc["B"]["qkT"][n % 2]
                    tp, btp = next_bank()
                    tpb = tp[:, 0:256].bitcast(BF16)
                    for i in range(2):
                        P.op("pe", lambda e, a=tpb[:, i * 128:(i + 1) * 128], b=qd[:, i * 128:(i + 1) * 128]:
                             e.transpose(out=a, in_=b, identity=ident[:, :]), [bqd, Bc], [btp])
                        P.op("pe", lambda e, a=tpb[:, (2 + i) * 128:(3 + i) * 128], b=kd[:, i * 128:(i + 1) * 128]:
                             e.transpose(out=a, in_=b, identity=ident[:, :]), [bkd, Bc], [btp])
                    act(qkT[:, :], tpb, AF.Copy, [btp], [bqkT])

                def st_d(d, n):
                    sc = SC[d]
                    qkT, bqkT = sc["qkT"][n % 2], sc["B"]["qkT"][n % 2]
                    Am, bAm = sc["Am"][n % 2], sc["B"]["Am"][n % 2]
                    apsb = [next_bank(), next_bank()]
                    Amv = Am[:, :].rearrange("p (i b t) -> p i b t", i=2, b=2)
                    for b in range(2):
                        aps, ba = apsb[b]
                        base = 64 * b
                        for i in range(2):
                            mm(aps[:, i * 128:(i + 1) * 128], qkT[base:base + 64, (2 + i) * 128:(3 + i) * 128],
                               qkT[base:base + 64, i * 128:(i + 1) * 128], True, True, [bqkT], [ba])
                    for b in range(2):
                        aps, ba = apsb[b]
                        vtt(Amv[:, :, b, :], aps[:, 0:256].rearrange("p (i t) -> p i t", i=2), mask4[:, d, 0:2, :], ALU.mult,
                            [ba, Bc], [bAm])

                def st_e(d, n):
                    sc = SC[d]; c = sc["order"][n]; ti = min(c // 4, 4)
                    kd, bkd = sc["kd"][n % 3], sc["B"]["kd"][n % 3]
                    qkT, bqkT = sc["qkT"][n % 2], sc["B"]["qkT"][n % 2]
                    Am, bAm = sc["Am"][n % 2], sc["B"]["Am"][n % 2]
                    eb, beb = sc["ebend"][n % 3], sc["B"]["ebend"][n % 3]
                    st_f, st_t, stb = sc["st_f"], sc["st_t"], sc["st_b"]
                    opsb = [next_bank(), next_bank()]
                    for b in range(2):
                        ops_, bo = opsb[b]
                        base = 64 * b
                        for i in range(2):
                            h = 2 * i + b
                            mm(ops_[0:96, i * 128:(i + 1) * 128], vg[:, c, h * 96:(h + 1) * 96], Am[:, h * 128:(h + 1) * 128], True, False,
                               [Bvg[c], bAm], [bo])
                            mm(ops_[0:96, i * 128:(i + 1) * 128], stb[base:base + 64, i * 96:(i + 1) * 96],
                               qkT[base:base + 64, i * 128:(i + 1) * 128], False, True, [sc["Bst_b"], bqkT], [bo])
                    ups, bu = next_bank()
                    for h in range(4):
                        i, base = h // 2, 64 * (h % 2)
                        mm(ups[base:base + 64, i * 96:(i + 1) * 96], kd[:, h * 64:(h + 1) * 64], vg[:, c, h * 96:(h + 1) * 96],
                           True, True, [bkd, Bvg[c]], [bu])
                    vtt(st_t[:, :], ups[:, 0:192], st_f[:, :], ALU.add, [bu, sc["Bst_f"]], [sc["Bst_t"]])
                    for i in range(2):
                        act(stb[:, i * 96:(i + 1) * 96], st_t[:, i * 96:(i + 1) * 96], AF.Identity, [sc["Bst_t"], beb], [sc["Bst_b"]],
                            scale=eb[:, i:i + 1])
                    for i in range(2):
                        vts(st_f[:, i * 96:(i + 1) * 96], st_t[:, i * 96:(i + 1) * 96], eb[:, i:i + 1], ALU.mult,
                            [sc["Bst_t"], beb], [sc["Bst_f"]])
                    oav = oacc[:, :, c * 128:(c + 1) * 128].rearrange("p (i b) t -> p i b t", b=2)
                    totv = tot[:, :].rearrange("p (i b t) -> p i b t", i=2, b=2)
                    if c not in stored:
                        stored.add(c)
                        for b in range(2):
                            ops_, bo = opsb[b]
                            ov = ops_[0:96, 0:256].rearrange("p (i t) -> p i t", i=2)
                            if b == 0:
                                act(oav[:, :, b, :], ov, AF.Copy, [bo], [Boa[c]])
                            else:
                                vcopy(oav[:, :, b, :], ov, [bo], [Boa[c]])
                    elif not (last and c >= 16):
                        for b in range(2):
                            ops_, bo = opsb[b]
                            ov = ops_[0:96, 0:256].rearrange("p (i t) -> p i t", i=2)
                            vtt(totv[:, :, b, :], ov, oav[:, :, b, :], ALU.add, [bo, Boa[c]], [Btot])
                        act(sqo[:, :], tot[:, :], AF.Square, [Btot], [Bsqo])
                        sps, bs_ = next_bank()
                        mm(sps[0:96, :], ones_bf[0:96, 0:96], sqo[:, :], True, True, [Bsqo, Bc], [bs_])
                        act(sps[0:96, :], sps[0:96, :], AF.Sqrt, [bs_], [bs_], bias=EPS, scale=1.0 / 96)
                        vrecip(sps[0:96, :], sps[0:96, :], [bs_], [bs_])
                        vtt(tot[:, :], tot[:, :], sps[0:96, :], ALU.mult, [Btot, bs_], [Btot])
                        mv = mixg[0:96, :, c * 128:(c + 1) * 128]
                        vstt(mv, tot[:, :].rearrange("p (h t) -> p h t", h=4), gnw[:, 0:1], mv, ALU.mult, ALU.mult,
                             [Btot, Bmod] + [Bm[h][ti] for h in range(4)], [Bm[h][ti] for h in range(4)])

                def ok(n):
                    return 0 <= n < NT
                for t in range(-3, NT):
                    for fn, dn in ((st_a, 3), (st_b, 2), (st_c, 1), (st_e, 0), (st_d, 1)):
                        for d in range(2):
                            if ok(t + dn) and ('C' + fn.__name__[-1]) not in skip:
                                fn(d, t + dn)
                P.barrier()
                if l == 0:
                    tap("mixC", mixg[0:96, :, :].rearrange("p c n -> p (c n)"), [96, 4 * NTOK], [Bm[c][t] for c in range(4) for t in range(5)], BF16)
                P.phase = 'L%d.C.out' % l
                out_proj(l, 640, 96, 4, None, tis)
                if l == n_layers - 1:
                    P.barrier()
            if l == 0:
                tap("x1", xT[:, :, :].rearrange("p c n -> p (c n)"), [128, 8 * NTOK], [Bx[c][t] for c in range(8) for t in range(5)])

        P.phase = 'final'
        sq = carve(0, 8 * 512, BF16).rearrange("p (c n) -> p c n", c=8)
        rstd = carve(8192, 512, F32)
        ost = [carve(10240 + 2048 * i, 512, F32) for i in range(4)]
        Bsq, Brs, Bos_ = Buf("sq"), Buf("rstd"), [Buf() for _ in range(4)]
        for ti in range(4):
            t0, tw, v = TBS[ti]
            for c in range(8):
                act(sq[:, c, 0:tw], xT[:, c, t0:t0 + tw], AF.Square, [Bx[c][ti]], [Bsq])
            ps, bps = next_bank()
            for c in range(8):
                mm(ps[:, 0:tw], ones_bf[:, :], sq[:, c, 0:tw], c == 0, c == 7, [Bsq, Bc], [bps])
            act(rstd[:, 0:tw], ps[:, 0:tw], AF.Sqrt, [bps], [Brs], bias=EPS, scale=1.0 / D)
            vrecip(rstd[:, 0:tw], rstd[:, 0:tw], [Brs], [Brs])
            for c in range(8):
                k = c % 4
                vstt(ost[k][:, 0:tw], xT[:, c, t0:t0 + tw], normf[:, c:c + 1], rstd[:, 0:tw], ALU.mult, ALU.mult,
                     [Bx[c][ti], Brs, Bc], [Bos_[k]])
                P.dma("sp", out_d[c * 128:(c + 1) * 128, t0:t0 + tw], ost[k][:, 0:tw], reads=[Bos_[k]])
        stats = P.emit(nc)
    _NC_CACHE['prog'] = P
    return nc, stats, tap_d


_NC_CACHE = {}


def prep_inputs(x, c, ctx, c_ctx, w_ada, b_ada, norm_w, w_in, w_four, rpb, w_alpha_fwd, b_alpha_fwd,
                w_alpha_bwd, b_alpha_bwd, gla_norm_w, w_out, norm_f, cores):
    C = _consts()
    f32 = np.float32
    x = np.asarray(x, f32); ctx = np.asarray(ctx, f32); c = np.asarray(c, f32); c_ctx = np.asarray(c_ctx, f32)
    rpb = np.asarray(rpb, f32)
    nl = rpb.shape[0]
    rpb_pad = np.concatenate([rpb.reshape(nl, 6, -1), np.full((nl, 6, 1), PAD_BIAS, f32)], axis=-1)
    biasT = rpb_pad[:, :, C["naidx"]]
    biasT = np.ascontiguousarray(biasT.transpose(0, 1, 3, 2, 4))
    walpha = np.zeros((nl, 33, 2, 192), f32)
    walpha[:, 0:16, 0, :] = np.asarray(w_alpha_fwd, f32)
    walpha[:, 16:32, 1, :] = np.asarray(w_alpha_bwd, f32)
    walpha[:, 32, 0, :] = np.asarray(b_alpha_fwd, f32)
    walpha[:, 32, 1, :] = np.asarray(b_alpha_bwd, f32)
    w_in = np.asarray(w_in, f32); w_ada = np.asarray(w_ada, f32); w_out = np.asarray(w_out, f32)
    pieces, _lay = _win_pieces()
    w_in_r = np.empty((nl, 128, 8 * DIN), f32)
    w_ada_r = np.empty((nl, 128, 8 * 3 * D), f32)
    w_out_r = np.zeros((nl, 128, 9216), f32)
    for l_ in range(nl):
        Wl = w_in[l_].reshape(8, 128, DIN).transpose(1, 0, 2)
        o_ = 0
        for p_ in pieces:
            blk = np.concatenate([Wl[:, :, c0:c0 + k] for c0, k in p_], axis=2)
            w_in_r[l_, :, o_:o_ + blk.shape[1] * blk.shape[2]] = blk.reshape(128, -1)
            o_ += blk.shape[1] * blk.shape[2]
        Wa = w_ada[l_].reshape(8, 128, 3 * D).transpose(1, 0, 2)
        for pj in range(8):
            w_ada_r[l_, :, pj * 3072:(pj + 1) * 3072] = Wa[:, :, pj * 384:(pj + 1) * 384].reshape(128, 3072)
        w_out_r[l_, :, 0:2048] = w_out[l_, 0:256].reshape(2, 128, D).transpose(1, 0, 2).reshape(128, 2048)
        w_out_r[l_, :, 2048:5120] = w_out[l_, 256:640].reshape(3, 128, D).transpose(1, 0, 2).reshape(128, 3072)
        Wc = w_out[l_, 640:1024].reshape(4, 96, D).transpose(1, 0, 2)
        w_out_r[l_, 0:96, 5120:7168] = Wc[:, :, 0:512].reshape(96, 2048)
        w_out_r[l_, 0:96, 7168:9216] = Wc[:, :, 512:1024].reshape(96, 2048)
    shared = {
        "w_ada_r": w_ada_r,
        "b_ada": np.ascontiguousarray(np.asarray(b_ada, f32).reshape(nl, 24, 128).transpose(0, 2, 1)),
        "norm_w": np.ascontiguousarray(np.asarray(norm_w, f32).reshape(nl, 8, 128).transpose(0, 2, 1)),
        "w_in_r": w_in_r,
        "w_four": np.ascontiguousarray(np.asarray(w_four, f32)),
        "biasT": biasT,
        "walpha": walpha.reshape(nl, 33, 384),
        "gnw": np.ascontiguousarray(np.asarray(gla_norm_w, f32).reshape(nl, 96, 1)),
        "w_out_r": w_out_r,
        "normf": np.ascontiguousarray(np.asarray(norm_f, f32).reshape(8, 128).T),
        "ident": C["ident"], "dft64": C["dft64"], "dftL": C["dftL"], "dftC": C["dftC"],
        "ropeC": C["ropeC"], "ropeS": C["ropeS"], "tri": C["tri"], "mask4": C["mask4"],
    }
    in_maps = []
    for b in cores:
        xT = np.ascontiguousarray(np.concatenate([x[b], ctx[b]], axis=0).T)
        cv = np.stack([c[b].reshape(8, 128).T, c_ctx.reshape(8, 128).T], axis=-1)
        m = dict(shared)
        m["xT"] = xT
        m["cvec"] = np.ascontiguousarray(cv.reshape(128, 16))
        in_maps.append(m)
    return in_maps


def kernel(**inputs):
    if "nc" not in _NC_CACHE:
        _NC_CACHE["nc"] = build()[0]
    nc = _NC_CACHE["nc"]
    in_maps = prep_inputs(cores=list(range(8)), **inputs)
    res = run_bass_kernel_spmd(nc, in_maps, core_ids=list(range(8)))
    out = np.stack([np.asarray(r["outT"]).T for r in res.results], axis=0)
    return np.ascontiguousarray(out.astype(np.float32))
```
